# Optimizing a Trainium2 kernel written in Bass

```python
import jax, jax.numpy as jnp
from jax import lax
import numpy as np

D_MODEL = 2048
BATCH = 8
SEQ = 2048
DEPTH = 2

D_MIX = D_MODEL
HEAD_DIM = 64
ATTN_WIDTH = D_MIX // 2
N_Q_HEADS = ATTN_WIDTH // HEAD_DIM
N_KV_HEADS = N_Q_HEADS // 4
KV_WIDTH = N_KV_HEADS * HEAD_DIM
WINDOW = 128
CONV_WIDTH = D_MIX // 4
CONV_KERNEL = 31
SGU_WIDTH = D_MIX // 4
SGU_HEADS = SGU_WIDTH // HEAD_DIM
CHUNK = 128
D_FF = 4 * D_MODEL
EPS = 1e-6
NEG_INF = -1e30
SPLITS = (ATTN_WIDTH,
          ATTN_WIDTH + KV_WIDTH,
          ATTN_WIDTH + 2 * KV_WIDTH,
          ATTN_WIDTH + 2 * KV_WIDTH + 2 * CONV_WIDTH)
D_IN = ATTN_WIDTH + 2 * KV_WIDTH + 2 * CONV_WIDTH + 2 * SGU_WIDTH

kernel_name = "hymba_conv_sgu_swa_hybrid"


def rms_norm(x, g):
    xf = x.astype(jnp.float32)
    y = xf * lax.rsqrt(jnp.mean(xf * xf, axis=-1, keepdims=True) + EPS)
    return (y * g.astype(jnp.float32)).astype(x.dtype)


def layer_norm(x, g, b):
    xf = x.astype(jnp.float32)
    mu = jnp.mean(xf, axis=-1, keepdims=True)
    xc = xf - mu
    y = xc * lax.rsqrt(jnp.mean(xc * xc, axis=-1, keepdims=True) + EPS)
    return (y * g.astype(jnp.float32) + b.astype(jnp.float32)).astype(x.dtype)


def sliding_window_attention(q, k, v, sinks):
    B, S = q.shape[0], q.shape[1]
    nb = S // WINDOW
    G = N_Q_HEADS // N_KV_HEADS
    qb = q.reshape(B, nb, WINDOW, N_KV_HEADS, G, HEAD_DIM)

    def with_prev(t):
        tb = t.reshape(B, nb, WINDOW, N_KV_HEADS, HEAD_DIM)
        prev = jnp.pad(tb, ((0, 0), (1, 0), (0, 0), (0, 0), (0, 0)))[:, :-1]
        return jnp.concatenate([prev, tb], axis=2)

    kb, vb = with_prev(k), with_prev(v)
    scale = HEAD_DIM ** -0.5
    logits = jnp.einsum('bnqkgd,bnskd->bnkgqs', qb, kb).astype(jnp.float32) * scale
    qi = jnp.arange(WINDOW)[None, :, None]
    sj = jnp.arange(2 * WINDOW)[None, None, :]
    blk = jnp.arange(nb)[:, None, None]
    rel = qi + WINDOW - sj
    key_pos = blk * WINDOW - WINDOW + sj
    mask = (rel >= 0) & (rel < WINDOW) & (key_pos >= 0)
    logits = jnp.where(mask[None, :, None, None], logits, NEG_INF)
    sink = sinks.astype(jnp.float32).reshape(N_KV_HEADS, G)[None, None, :, :, None, None]
    m = jnp.maximum(jnp.max(logits, axis=-1, keepdims=True), sink)
    p = jnp.exp(logits - m)
    denom = jnp.sum(p, axis=-1, keepdims=True) + jnp.exp(sink - m)
    probs = (p / denom).astype(v.dtype)
    out = jnp.einsum('bnkgqs,bnskd->bnqkgd', probs, vb)
    return out.reshape(B, S, ATTN_WIDTH)


def conv_module(xc, conv_w, conv_b, ln_g, ln_b):
    a, gate = jnp.split(xc, 2, axis=-1)
    h = a * jax.nn.sigmoid(gate)
    h = lax.conv_general_dilated(
        h, conv_w[:, None, :].astype(h.dtype), window_strides=(1,),
        padding=[(CONV_KERNEL - 1, 0)],
        dimension_numbers=('NWC', 'WIO', 'NWC'),
        feature_group_count=CONV_WIDTH) + conv_b
    h = layer_norm(h, ln_g, ln_b)
    return jax.nn.silu(h)


def spatial_gating(xs, ln_g, ln_b, w_s, b_s):
    B, S = xs.shape[0], xs.shape[1]
    u, v = jnp.split(xs, 2, axis=-1)
    v = layer_norm(v, ln_g, ln_b)
    vb = v.reshape(B, S // CHUNK, CHUNK, SGU_HEADS, HEAD_DIM)
    causal = jnp.tril(jnp.ones((CHUNK, CHUNK), dtype=bool))
    w = jnp.where(causal[None], w_s, jnp.zeros_like(w_s))
    s = jnp.einsum('hij,bnjhd->bnihd', w, vb) + b_s.T[None, None, :, :, None]
    return u * s.reshape(B, S, SGU_WIDTH)


def setup_inputs(seed: int = 0) -> dict:
    key = jax.random.key(seed)
    ks = jax.random.split(key, 20)
    f32 = jnp.float32

    def nrm(k, shape, scale):
        return jax.random.normal(k, shape, f32) * scale

    def gain(k, shape):
        return 1.0 + 0.02 * jax.random.normal(k, shape, f32)

    return {
        "x": jax.random.normal(ks[0], (BATCH, SEQ, D_MODEL), f32),
        "ln1_g": gain(ks[1], (DEPTH, D_MODEL)),
        "w_in": nrm(ks[2], (DEPTH, D_MODEL, D_IN), D_MODEL ** -0.5),
        "q_norm_g": gain(ks[3], (DEPTH, HEAD_DIM)),
        "k_norm_g": gain(ks[4], (DEPTH, HEAD_DIM)),
        "sinks": nrm(ks[5], (DEPTH, N_Q_HEADS), 0.5),
        "conv_w": nrm(ks[6], (DEPTH, CONV_KERNEL, CONV_WIDTH), CONV_KERNEL ** -0.5),
        "conv_b": nrm(ks[7], (DEPTH, CONV_WIDTH), 0.02),
        "conv_ln_g": gain(ks[8], (DEPTH, CONV_WIDTH)),
        "conv_ln_b": nrm(ks[9], (DEPTH, CONV_WIDTH), 0.02),
        "sgu_ln_g": gain(ks[10], (DEPTH, SGU_WIDTH)),
        "sgu_ln_b": nrm(ks[11], (DEPTH, SGU_WIDTH), 0.02),
        "sgu_w": nrm(ks[12], (DEPTH, SGU_HEADS, CHUNK, CHUNK), CHUNK ** -0.5),
        "sgu_b": gain(ks[13], (DEPTH, SGU_HEADS, CHUNK)),
        "out_norm_g": gain(ks[14], (DEPTH, D_MIX)),
        "w_out": nrm(ks[15], (DEPTH, D_MIX, D_MODEL), D_MIX ** -0.5),
        "ln2_g": gain(ks[16], (DEPTH, D_MODEL)),
        "w_up": nrm(ks[17], (DEPTH, D_MODEL, D_FF), D_MODEL ** -0.5),
        "w_down": nrm(ks[18], (DEPTH, D_FF, D_MODEL), D_FF ** -0.5),
    }


def reference(x, ln1_g, w_in, q_norm_g, k_norm_g, sinks, conv_w, conv_b, conv_ln_g,
              conv_ln_b, sgu_ln_g, sgu_ln_b, sgu_w, sgu_b, out_norm_g, w_out, ln2_g,
              w_up, w_down):
    B, S = x.shape[0], x.shape[1]
    for l in range(DEPTH):
        h = rms_norm(x, ln1_g[l])
        proj = h @ w_in[l]
        q, k, v, xc, xs = jnp.split(proj, SPLITS, axis=-1)
        q = rms_norm(q.reshape(B, S, N_Q_HEADS, HEAD_DIM), q_norm_g[l])
        k = rms_norm(k.reshape(B, S, N_KV_HEADS, HEAD_DIM), k_norm_g[l])
        v = v.reshape(B, S, N_KV_HEADS, HEAD_DIM)
        y_attn = sliding_window_attention(q, k, v, sinks[l])
        y_conv = conv_module(xc, conv_w[l], conv_b[l], conv_ln_g[l], conv_ln_b[l])
        y_sgu = spatial_gating(xs, sgu_ln_g[l], sgu_ln_b[l], sgu_w[l], sgu_b[l])
        g = out_norm_g[l]
        mix = jnp.concatenate([
            rms_norm(y_attn, g[:ATTN_WIDTH]),
            rms_norm(y_conv, g[ATTN_WIDTH:ATTN_WIDTH + CONV_WIDTH]),
            rms_norm(y_sgu, g[ATTN_WIDTH + CONV_WIDTH:]),
        ], axis=-1)
        x = x + mix @ w_out[l]
        h = rms_norm(x, ln2_g[l])
        x = x + jnp.square(jax.nn.relu(h @ w_up[l])) @ w_down[l]
    return x
```

```python
import contextlib
import numpy as np
import concourse.bass as bass
import concourse.mybir as mybir
from concourse.bass_utils import run_bass_kernel_spmd

F32 = mybir.dt.float32
BF16 = mybir.dt.bfloat16
AF = mybir.ActivationFunctionType
ALU = mybir.AluOpType

P = 128
D = 2048
KC = D // P
DFF = 8192
FC = DFF // P
DIN = 3584
T = 512
NBLK = T // P
EPS = 1e-6
CONVK = 31
NSLAB = 43
SLAB_ELEMS = 8192
SAME_ENGINE_SYNC = True

OFF = {}
_o = 0
for _nm, _w in (("g1", 16), ("g2", 16), ("go", 16), ("gq", 1), ("gk", 1), ("sink", 8),
                ("convw", 4 * CONVK), ("convb", 4), ("clng", 4), ("clnb", 4)):
    OFF[_nm] = _o
    _o += _w
NCV = _o


class Sched:
    ENGS = ("pe", "act", "dve", "pool", "sp")

    def __init__(self):
        self.ops = []
        self.by_eng = {e: [] for e in self.ENGS}
        self.state = {}
        self.alias = {}
        self.dma_val = {}
        self.dma_last = {}

    def set_alias_groups(self, regions):
        names = list(regions)
        for a in names:
            sa, ea = regions[a]
            self.alias[a] = [b for b in names if regions[b][0] < ea and sa < regions[b][1]]

    def _al(self, b):
        return self.alias.get(b, (b,))

    def op(self, eng, fn, r=(), w=(), dma=None, ndma=1):
        oid = len(self.ops)
        deps = {}
        for b in r:
            for a in self._al(b):
                st = self.state.get(a)
                if st and st[0] is not None:
                    deps[st[0]] = True
        for b in w:
            for a in self._al(b):
                st = self.state.get(a)
                if st:
                    if st[0] is not None:
                        deps.setdefault(st[0], False)
                    for o_ in st[1].values():
                        deps.setdefault(o_, False)
        rec = dict(id=oid, eng=eng, fn=fn, deps=deps, dma=dma, ndma=ndma, target=False)
        if dma is not None:
            if dma in self.dma_last:
                deps[self.dma_last[dma]] = True
            self.dma_last[dma] = oid
            v = self.dma_val.get(dma, 0) + 16 * ndma
            self.dma_val[dma] = v
            rec["sig"] = (dma, v)
        self.ops.append(rec)
        self.by_eng[eng].append(rec)
        sk = dma if dma is not None else eng
        for b in r:
            st = self.state.setdefault(b, [None, {}])
            st[1][sk] = oid
        for b in w:
            self.state[b] = [oid, {}]
        return oid

    def _needs_wait(self, r, dr, raw=True):
        if dr["dma"] is not None:
            return True
        if r["dma"] is None and dr["eng"] == r["eng"]:
            if dr["eng"] == "pe":
                return False
            return SAME_ENGINE_SYNC and raw
        return True

    def finalize(self):
        for r in self.ops:
            for d, raw in r["deps"].items():
                dr = self.ops[d]
                if dr["dma"] is None and self._needs_wait(r, dr, raw):
                    dr["target"] = True
        cnt = {e: 0 for e in self.ENGS}
        for r in self.ops:
            if r["dma"] is None and r["target"]:
                cnt[r["eng"]] += 1
                r["sig"] = (r["eng"], cnt[r["eng"]])
        self.counts = cnt

    def emit_engine(self, ename, eng, sems):
        waited = {}
        for r in self.by_eng[ename]:
            need = {}
            for d, raw in r["deps"].items():
                dr = self.ops[d]
                if not self._needs_wait(r, dr, raw):
                    continue
                sk, v = dr["sig"]
                if need.get(sk, 0) < v:
                    need[sk] = v
            for sk, v in need.items():
                if waited.get(sk, 0) < v:
                    eng.wait_ge(sems[sk], v)
                    waited[sk] = v
            if r["fn"] is None:
                continue
            ins = r["fn"](eng)
            if "sig" in r:
                sk = r["sig"][0]
                if r["dma"] is not None:
                    assert isinstance(ins, (list, tuple)) and len(ins) == r["ndma"]
                    for i_ in ins:
                        i_.then_inc(sems[sk], 16)
                else:
                    ins.then_inc(sems[sk], 1)


def build_nc(S, L, debug=False):
    NT = S // T
    dbg_list = []
    nc = bass.Bass("TRN2", target_bir_lowering=False)
    xT = nc.dram_tensor("xT", [D, S], F32, kind="ExternalInput").ap()
    w_in = nc.dram_tensor("w_in", [L, D, DIN], F32, kind="ExternalInput").ap()
    w_out = nc.dram_tensor("w_out", [L, D, D], F32, kind="ExternalInput").ap()
    w_up = nc.dram_tensor("w_up", [L, D, DFF], F32, kind="ExternalInput").ap()
    w_down = nc.dram_tensor("w_down", [L, DFF, D], F32, kind="ExternalInput").ap()
    cvec = nc.dram_tensor("cvec", [P, L * NCV], F32, kind="ExternalInput").ap()
    cbc = nc.dram_tensor("cbc", [P, L * 2 * 512], F32, kind="ExternalInput").ap()
    cbs = nc.dram_tensor("cbs", [P, L * 4 * 128], F32, kind="ExternalInput").ap()
    cws = nc.dram_tensor("cws", [P, L * 8 * 128], F32, kind="ExternalInput").ap()
    cmat = nc.dram_tensor("cmat", [P, 1152], F32, kind="ExternalInput").ap()
    yT = nc.dram_tensor("yT", [D, S], F32, kind="ExternalOutput").ap()
    scr = nc.dram_tensor("wscr", [L, NSLAB, P, SLAB_ELEMS], BF16, kind="Internal").ap()

    S_ = Sched()
    es = contextlib.ExitStack()

    def sb(name, shape, dt):
        return es.enter_context(nc.sbuf_tensor(name, shape, dt))

    x_sb = sb("x_sb", [P, KC, T], F32)
    act_sb = sb("act_sb", [P, KC, T], BF16)
    slab = [sb(f"slab{i}", [P, SLAB_ELEMS], BF16) for i in range(2)]
    cv_sb = sb("cv_sb", [P, L * NCV], F32)
    gbc_sb = sb("gbc_sb", [P, L * 2, 512], F32)
    bs_sb = sb("bs_sb", [P, L * 4, 128], F32)
    ws_bf = sb("ws_bf", [P, L * 8, 128], BF16)
    mask_bf = sb("mask_bf", [P, 2, 512], BF16)
    perm_bf = sb("perm_bf", [P, P], BF16)
    ones_bf = sb("ones_bf", [P, P], BF16)
    bd_bf = sb("bd_bf", [P, P], BF16)
    esink = sb("esink", [P, L * 8], F32)
    kcar = [sb(f"kcar{l}", [P, 4, P], BF16) for l in range(L)]
    vcar = [sb(f"vcar{l}", [P, 256], BF16) for l in range(L)]
    hcar = [sb(f"hcar{l}", [P, 4, CONVK - 1], F32) for l in range(L)]
    small = sb("small", [P, 64], F32)

    UB = 88064
    U = sb("U", [P, UB // 4], F32)
    regions = {}

    def carve(name, off, nbytes, dt, pattern=None, **kw):
        assert off % 4 == 0 and nbytes % 4 == 0
        ap = U[:, off // 4:(off + nbytes) // 4]
        if dt == BF16:
            ap = ap.bitcast(BF16)
        if pattern:
            ap = ap.rearrange(pattern, **kw)
        return ap

    o = 0
    hid = carve("hid", 0, 65536, BF16, "p (j t) -> p j t", t=T)
    for j in range(FC):
        regions[("hid", j)] = (j * 1024, (j + 1) * 1024)
    qn = carve("qn", o, 8192, BF16, "p (c t) -> p c t", t=T)
    for c in range(8):
        regions[("qn", c)] = (o + c * 1024, o + (c + 1) * 1024)
    o += 8192
    kn = carve("kn", o, 4096, BF16, "p (c t) -> p c t", t=T)
    for c in range(4):
        regions[("kn", c)] = (o + c * 1024, o + (c + 1) * 1024)
    o += 4096
    vt = carve("vt", o, 2048, BF16, "p (b d) -> p b d", d=256)
    for b in range(4):
        regions[("vt", b)] = (o + b * 512, o + (b + 1) * 512)
    o += 2048
    svn = carve("svn", o, 4096, BF16, "p (b d) -> p b d", d=512)
    for b in range(4):
        regions[("svn", b)] = (o + b * 1024, o + (b + 1) * 1024)
    o += 4096
    u_sb = carve("u", o, 8192, F32, "p (c t) -> p c t", t=T)
    for c in range(4):
        regions[("u", c)] = (o + c * 2048, o + (c + 1) * 2048)
    o += 8192
    HB = T + CONVK - 1
    hbuf = carve("hbuf", o, 4 * HB * 4, F32, "p (c t) -> p c t", t=HB)
    for c in range(4):
        regions[("hbuf", c)] = (o + c * HB * 4, o + (c + 1) * HB * 4)
    o += 4 * HB * 4
    cc = carve("cc", o, 8192, F32, "p (c t) -> p c t", t=T)
    for c in range(4):
        regions[("cc", c)] = (o + c * 2048, o + (c + 1) * 2048)
    o += 8192
    ya = carve("ya", o, 16384, F32, "p (c t) -> p c t", t=T)
    for c in range(8):
        regions[("ya", c)] = (o + c * 2048, o + (c + 1) * 2048)
    o += 16384
    assert o <= 65536, o
    stg_ws = carve("stg_ws", 0, L * 8 * 128 * 4, F32)
    regions["stg_ws"] = (0, L * 8 * 128 * 4)
    stg_cm = carve("stg_cm", 16384, 1152 * 4, F32)
    regions["stg_cm"] = (16384, 16384 + 1152 * 4)
    o = 65536
    NTF = 6
    tmpf = carve("tmpf", o, NTF * 2048, F32, "p (s t) -> p s t", t=T)
    for s in range(NTF):
        regions[("tmpf", s)] = (o + s * 2048, o + (s + 1) * 2048)
    o += NTF * 2048
    NSQ = 4
    sqb = carve("sqb", o, NSQ * 1024, BF16, "p (s t) -> p s t", t=T)
    for s in range(NSQ):
        regions[("sqb", s)] = (o + s * 1024, o + (s + 1) * 1024)
    o += NSQ * 1024
    NPT = 6
    ptb = carve("ptb", o, NPT * 1024, BF16, "p (s t) -> p s t", t=T)
    for s in range(NPT):
        regions[("ptb", s)] = (o + s * 1024, o + (s + 1) * 1024)
    o += NPT * 1024
    assert o <= UB, o
    S_.set_alias_groups(regions)

    ps = [es.enter_context(nc.psum_tensor(f"ps{i}", [P, 512], F32)) for i in range(8)]

    ctr = dict(bank=0, tmpf=0, sqb=0, ptb=0, slab=0, small=0)

    def nxt(kind, n):
        v = ctr[kind]
        ctr[kind] = (v + 1) % n
        return v

    RMS_BANK = 7

    def next_bank():
        return nxt("bank", 7)

    def cvl(l, name, i=0):
        c0 = l * NCV + OFF[name] + i
        return cv_sb[:, c0:c0 + 1]

    A = S_.op
    pstate = {}

    def dump(name, ap, shape, dt, bufs):
        if not debug:
            return
        d = nc.dram_tensor("dbg_" + name, list(shape), dt, kind="ExternalOutput").ap()
        dbg_list.append(name)
        A("sp", lambda e: [e.dma_start(out=d, in_=ap)], r=bufs, dma="dbg_" + name)

    A("sp", lambda e: [e.dma_start(out=cv_sb[:], in_=cvec)], w=["cv"], dma="cst")
    A("sp", lambda e: [e.dma_start(out=gbc_sb[:].rearrange("p a b -> p (a b)"), in_=cbc)], w=["gbc"], dma="cst")
    A("sp", lambda e: [e.dma_start(out=bs_sb[:].rearrange("p a b -> p (a b)"), in_=cbs)], w=["bs"], dma="cst")
    A("sp", lambda e: [e.dma_start(out=stg_ws, in_=cws)], w=["stg_ws"], dma="cst")
    A("sp", lambda e: [e.dma_start(out=stg_cm, in_=cmat)], w=["stg_cm"], dma="cst")
    A("dve", lambda e: e.memset(ones_bf[:], 1.0), w=["ones"])
    A("dve", lambda e: e.memset(bd_bf[:], 0.0), w=["bd"])
    A("dve", lambda e: e.memset(bd_bf[0:64, 0:64], 1.0), w=["bd"])
    A("dve", lambda e: e.memset(bd_bf[64:128, 64:128], 1.0), w=["bd"])
    A("dve", lambda e: e.tensor_copy(out=mask_bf[:].rearrange("p a b -> p (a b)"), in_=stg_cm[:, 0:1024]),
      r=["stg_cm"], w=["mask"])
    A("dve", lambda e: e.tensor_copy(out=perm_bf[:], in_=stg_cm[:, 1024:1152]), r=["stg_cm"], w=["perm"])
    for i in range(L * 8):
        A("dve", lambda e, i=i: e.tensor_tensor(out=ws_bf[:, i, :], in0=stg_ws[:, i * 128:(i + 1) * 128],
                                               in1=stg_cm[:, 128:256], op=ALU.mult),
          r=["stg_ws", "stg_cm"], w=["ws"])
    for l in range(L):
        A("act", lambda e, l=l: e.activation(out=esink[:, l * 8:(l + 1) * 8],
                                             in_=cv_sb[:, l * NCV + OFF["sink"]:l * NCV + OFF["sink"] + 8],
                                             func=AF.Exp), r=["cv"], w=["esink"])
        A("dve", lambda e, l=l: e.memset(kcar[l][:], 0.0), w=[("kcar", l)])
        A("dve", lambda e, l=l: e.memset(vcar[l][:], 0.0), w=[("vcar", l)])
        A("dve", lambda e, l=l: e.memset(hcar[l][:], 0.0), w=[("hcar", l)])

    def slab_src(l, s):
        if s < 7:
            return w_in[l].rearrange("(kc p) m -> p kc m", p=P)[:, :, s * 512:(s + 1) * 512], 512
        if s < 11:
            return w_out[l].rearrange("(kc p) m -> p kc m", p=P)[:, :, (s - 7) * 512:(s - 6) * 512], 512
        if s < 27:
            return w_up[l].rearrange("(kc p) m -> p kc m", p=P)[:, :, (s - 11) * 512:(s - 10) * 512], 512
        return w_down[l].rearrange("(kc p) m -> p kc m", p=P)[:, :, (s - 27) * 128:(s - 26) * 128], 128

    W_IN_ORDER = [3, 4, 6, 5, 2, 0, 1]

    def load_slab(l, s, n):
        b = nxt("slab", 2)
        if n == 0:
            src_ap, mw = slab_src(l, s)
            dst = slab[b][:].rearrange("p (kc m) -> p kc m", m=mw)
            A("pool", lambda e, src_ap=src_ap, dst=dst: [e.dma_start(out=dst, in_=src_ap)],
              w=[("slab", b)], dma=f"slab{b}")
            if NT > 1:
                A("sp", lambda e, b=b, l=l, s=s: [e.dma_start(out=scr[l, s], in_=slab[b][:])],
                  r=[("slab", b)], w=[("scr", l, s)], dma=f"st{b}")
        else:
            A("sp", lambda e, b=b, l=l, s=s: [e.dma_start(out=slab[b][:], in_=scr[l, s])],
              r=[("scr", l, s)], w=[("slab", b)], dma=f"slab{b}")
        return b

    def rstd_from_bank(bank, width):
        s1 = nxt("tmpf", NTF)
        A("act", lambda e: e.activation(out=tmpf[:, s1, :], in_=ps[bank][:], func=AF.Sqrt,
                                        bias=eps_ap, scale=1.0 / width),
          r=[("ps", bank), "eps"], w=[("tmpf", s1)])
        s2 = nxt("tmpf", NTF)
        A("dve", lambda e: e.reciprocal(out=tmpf[:, s2, :], in_=tmpf[:, s1, :]),
          r=[("tmpf", s1)], w=[("tmpf", s2)])
        return s2

    def sumsq_accum(bank, src_ap, src_buf, lhsT_ap, lhsT_buf, first, last):
        q = nxt("sqb", NSQ)
        A("act", lambda e: e.activation(out=sqb[:, q, :], in_=src_ap, func=AF.Square),
          r=[src_buf], w=[("sqb", q)])
        A("pe", lambda e: e.matmul(ps[bank][:], lhsT=lhsT_ap, rhs=sqb[:, q, :], start=first, stop=last),
          r=[("sqb", q), lhsT_buf], w=[("ps", bank)])

    def rms_accum(bank, kc):
        sumsq_accum(bank, x_sb[:, kc, :], ("x", kc), ones_bf[:], "ones", kc == 0, kc == KC - 1)

    def rmsnorm_to_act(l, gname, bank=None):
        if bank is None:
            bank = next_bank()
            for kc in range(KC):
                rms_accum(bank, kc)
        rr = rstd_from_bank(bank, float(D))
        for kc in range(KC):
            A("dve", lambda e, kc=kc: e.scalar_tensor_tensor(
                out=act_sb[:, kc, :], in0=x_sb[:, kc, :], scalar=cvl(l, gname, kc), in1=tmpf[:, rr, :],
                op0=ALU.mult, op1=ALU.mult),
              r=[("x", kc), ("tmpf", rr), "cv"], w=[("act", kc)])

    def group_norm_to_mix(l, ybuf, yname, nch, mix_off):
        bank = next_bank()
        for c in range(nch):
            sumsq_accum(bank, ybuf[:, c, :], (yname, c), ones_bf[:], "ones", c == 0, c == nch - 1)
        rr = rstd_from_bank(bank, float(nch * P))
        for c in range(nch):
            A("dve", lambda e, c=c: e.scalar_tensor_tensor(
                out=act_sb[:, mix_off + c, :], in0=ybuf[:, c, :], scalar=cvl(l, "go", mix_off + c),
                in1=tmpf[:, rr, :], op0=ALU.mult, op1=ALU.mult),
              r=[(yname, c), ("tmpf", rr), "cv"], w=[("act", mix_off + c)])

    def fm_group(bank, sbuf_i, kcn, mw, j, rhs_fn):
        sv = slab[sbuf_i][:].rearrange("p (kc m) -> p kc m", m=mw)

        def fn(e):
            ins = None
            for kc in range(kcn):
                ins = e.matmul(ps[bank][:], lhsT=sv[:, kc, j * P:(j + 1) * P], rhs=rhs_fn(kc)[0],
                               start=(kc == 0), stop=(kc == kcn - 1))
            return ins
        A("pe", fn, r=[("slab", sbuf_i)] + [rhs_fn(kc)[1] for kc in range(kcn)], w=[("ps", bank)])

    def act_rhs(kc):
        return act_sb[:, kc, :], ("act", kc)

    def hid_rhs(kc):
        return hid[:, kc, :], ("hid", kc)

    eps_ap = small[:, 0:1]
    A("dve", lambda e: e.memset(small[:, 0:1], EPS), w=["eps"])

    def step(n, l):
        t0 = n * T
        cgh = []
        pstate["proj_done"] = False

        def pump(k=1):
            for _ in range(k):
                if cgh:
                    next(cgh[0], None)
        if l == 0:
            for kc in range(KC):
                A("sp", lambda e, kc=kc: [e.dma_start(out=x_sb[:, kc, :], in_=xT[kc * P:(kc + 1) * P, t0:t0 + T])],
                  w=[("x", kc)], dma=f"xs{kc}")
        for c in range(4):
            A("dve", lambda e, c=c: e.tensor_copy(out=hbuf[:, c, 0:CONVK - 1], in_=hcar[l][:, c, :]),
              r=[("hcar", l)], w=[("hbuf", c)])
        rmsnorm_to_act(l, "g1", bank=(pstate.pop("rms1_bank") if l > 0 else None))

        for s in W_IN_ORDER:
            sbi = load_slab(l, s, n)
            if s == 3:
                for j in range(4):
                    bank = next_bank()
                    fm_group(bank, sbi, KC, 512, j, act_rhs)
                    A("act", lambda e, j=j, bank=bank: e.activation(out=hbuf[:, j, CONVK - 1:HB], in_=ps[bank][:], func=AF.Copy),
                      r=[("ps", bank)], w=[("hbuf", j)])
            elif s == 4:
                for j in range(4):
                    bank = next_bank()
                    fm_group(bank, sbi, KC, 512, j, act_rhs)
                    ts = nxt("tmpf", NTF)
                    A("act", lambda e, bank=bank, ts=ts: e.activation(out=tmpf[:, ts, :], in_=ps[bank][:], func=AF.Sigmoid),
                      r=[("ps", bank)], w=[("tmpf", ts)])
                    A("dve", lambda e, j=j, ts=ts: e.tensor_tensor(out=hbuf[:, j, CONVK - 1:HB], in0=hbuf[:, j, CONVK - 1:HB],
                                                                   in1=tmpf[:, ts, :], op=ALU.mult),
                      r=[("tmpf", ts), ("hbuf", j)], w=[("hbuf", j)])
                cgh.append(conv_chain(n, l))
            elif s == 6:
                sv = slab[sbi][:].rearrange("p (kc m) -> p kc m", m=512)
                for tb in range(NBLK):
                    bank = next_bank()

                    def fn(e, tb=tb, bank=bank, sv=sv):
                        ins = None
                        for kc in range(KC):
                            ins = e.matmul(ps[bank][:], lhsT=act_sb[:, kc, tb * P:(tb + 1) * P], rhs=sv[:, kc, :],
                                           start=(kc == 0), stop=(kc == KC - 1))
                        return ins
                    A("pe", fn, r=[("slab", sbi)] + [("act", kc) for kc in range(KC)], w=[("ps", bank)])
                    sm = nxt("small", 4)
                    st_ap = small[:, 8 + sm * 12:8 + sm * 12 + 6]
                    mv_ap = small[:, 8 + sm * 12 + 6:8 + sm * 12 + 8]
                    sd_ap = small[:, 8 + sm * 12 + 8:8 + sm * 12 + 9]
                    rs_ap = small[:, 8 + sm * 12 + 9:8 + sm * 12 + 10]
                    A("dve", lambda e, bank=bank, st_ap=st_ap: e.bn_stats(out=st_ap, in_=ps[bank][:]),
                      r=[("ps", bank)], w=[("sm", sm)])
                    A("dve", lambda e, st_ap=st_ap, mv_ap=mv_ap: e.bn_aggr(out=mv_ap, in_=st_ap), r=[("sm", sm)], w=[("sm", sm)])
                    A("act", lambda e, mv_ap=mv_ap, sd_ap=sd_ap: e.activation(out=sd_ap, in_=mv_ap[:, 1:2], func=AF.Sqrt, bias=eps_ap, scale=1.0),
                      r=[("sm", sm), "eps"], w=[("sm", sm)])
                    A("dve", lambda e, sd_ap=sd_ap, rs_ap=rs_ap: e.reciprocal(out=rs_ap, in_=sd_ap), r=[("sm", sm)], w=[("sm", sm)])
                    ts = nxt("tmpf", NTF)
                    A("dve", lambda e, bank=bank, ts=ts, mv_ap=mv_ap, rs_ap=rs_ap: e.tensor_scalar(
                        out=tmpf[:, ts, :], in0=ps[bank][:], scalar1=mv_ap[:, 0:1], scalar2=rs_ap,
                        op0=ALU.subtract, op1=ALU.mult),
                      r=[("ps", bank), ("sm", sm)], w=[("tmpf", ts)])
                    A("dve", lambda e, ts=ts: e.tensor_tensor(out=tmpf[:, ts, :], in0=tmpf[:, ts, :], in1=gbc_sb[:, l * 2, :], op=ALU.mult),
                      r=[("tmpf", ts), "gbc"], w=[("tmpf", ts)])
                    A("dve", lambda e, ts=ts, tb=tb: e.tensor_tensor(out=svn[:, tb, :], in0=tmpf[:, ts, :], in1=gbc_sb[:, l * 2 + 1, :], op=ALU.add),
                      r=[("tmpf", ts), "gbc"], w=[("svn", tb)])
                    pump()
            elif s == 5:
                for j in range(4):
                    bank = next_bank()
                    fm_group(bank, sbi, KC, 512, j, act_rhs)
                    A("act", lambda e, j=j, bank=bank: e.activation(out=u_sb[:, j, :], in_=ps[bank][:], func=AF.Copy),
                      r=[("ps", bank)], w=[("u", j)])
                    pump()
                sgu_chain(n, l)
            elif s == 2:
                for j in range(2):
                    bank = next_bank()
                    fm_group(bank, sbi, KC, 512, j, act_rhs)
                    head_norm(l, bank, "gk", kn[:, j, :], ("kn", j))
                    b2 = next_bank()
                    A("pe", lambda e, j=j, b2=b2: e.matmul(ps[b2][:], lhsT=perm_bf[:], rhs=kn[:, j, :], start=True, stop=True),
                      r=[("kn", j), "perm"], w=[("ps", b2)])
                    A("act", lambda e, j=j, b2=b2: e.activation(out=kn[:, 2 + j, :], in_=ps[b2][:], func=AF.Copy),
                      r=[("ps", b2)], w=[("kn", 2 + j)])
                    pump()
                sv = slab[sbi][:].rearrange("p (kc m) -> p kc m", m=512)
                for tb in range(NBLK):
                    bank = next_bank()

                    def fn(e, tb=tb, bank=bank, sv=sv):
                        ins = None
                        for kc in range(KC):
                            ins = e.matmul(ps[bank][:, 0:256], lhsT=act_sb[:, kc, tb * P:(tb + 1) * P], rhs=sv[:, kc, 256:512],
                                           start=(kc == 0), stop=(kc == KC - 1))
                        return ins
                    A("pe", fn, r=[("slab", sbi)] + [("act", kc) for kc in range(KC)], w=[("ps", bank)])
                    A("act", lambda e, tb=tb, bank=bank: e.activation(out=vt[:, tb, :], in_=ps[bank][:, 0:256], func=AF.Copy),
                      r=[("ps", bank)], w=[("vt", tb)])
                    pump()
            else:
                for j in range(4):
                    bank = next_bank()
                    fm_group(bank, sbi, KC, 512, j, act_rhs)
                    cq = s * 4 + j
                    head_norm(l, bank, "gq", qn[:, cq, :], ("qn", cq))
                    pump()
        pstate["proj_done"] = True
        group_norm_to_mix(l, u_sb, "u", 4, 12)
        attention(n, l, pump)
        pump(1000)
        if n == 0 and l == 0:
            dump("qn", qn, [P, 8, T], BF16, [("qn", c) for c in range(8)])
            dump("kn", kn, [P, 4, T], BF16, [("kn", c) for c in range(4)])
            dump("vt", vt, [P, 4, 256], BF16, [("vt", c) for c in range(4)])
            dump("ya", ya, [P, 8, T], F32, [("ya", c) for c in range(8)])
            dump("svn", svn, [P, 4, 512], BF16, [("svn", c) for c in range(4)])
        group_norm_to_mix(l, ya, "ya", 8, 0)
        if n == 0 and l == 0:
            dump("mix", act_sb[:], [P, KC, T], BF16, [("act", c) for c in range(KC)])

        rb2 = RMS_BANK
        for s in range(7, 11):
            sbi = load_slab(l, s, n)
            for j in range(4):
                m = (s - 7) * 4 + j
                bank = next_bank()
                fm_group(bank, sbi, KC, 512, j, act_rhs)
                A("dve", lambda e, m=m, bank=bank: e.tensor_tensor(out=x_sb[:, m, :], in0=ps[bank][:], in1=x_sb[:, m, :], op=ALU.add),
                  r=[("ps", bank), ("x", m)], w=[("x", m)])
                rms_accum(rb2, m)
        if n == 0 and l == 0:
            dump("x1", x_sb[:], [P, KC, T], F32, [("x", c) for c in range(KC)])
        rmsnorm_to_act(l, "g2", bank=rb2)
        for s in range(11, 27):
            sbi = load_slab(l, s, n)
            for j in range(4):
                f = (s - 11) * 4 + j
                bank = next_bank()
                fm_group(bank, sbi, KC, 512, j, act_rhs)
                ts = nxt("tmpf", NTF)
                A("act", lambda e, bank=bank, ts=ts: e.activation(out=tmpf[:, ts, :], in_=ps[bank][:], func=AF.Relu),
                  r=[("ps", bank)], w=[("tmpf", ts)])
                A("dve", lambda e, bank=bank, ts=ts, f=f: e.tensor_tensor(out=hid[:, f, :], in0=ps[bank][:], in1=tmpf[:, ts, :], op=ALU.mult),
                  r=[("ps", bank), ("tmpf", ts)], w=[("hid", f)])
        if l < L - 1:
            pstate["rms1_bank"] = RMS_BANK
        for s in range(27, 43):
            sbi = load_slab(l, s, n)
            m = s - 27
            bank = next_bank()
            fm_group(bank, sbi, FC, 128, 0, hid_rhs)
            A("dve", lambda e, m=m, bank=bank: e.tensor_tensor(out=x_sb[:, m, :], in0=ps[bank][:], in1=x_sb[:, m, :], op=ALU.add),
              r=[("ps", bank), ("x", m)], w=[("x", m)])
            if l == L - 1:
                A("act", lambda e, m=m: [e.dma_start(out=yT[m * P:(m + 1) * P, t0:t0 + T], in_=x_sb[:, m, :])],
                  r=[("x", m)], dma=f"xs{m}")
            else:
                rms_accum(pstate["rms1_bank"], m)

    def head_norm(l, bank, gname, out_ap, out_buf):
        b2 = next_bank()
        sumsq_accum(b2, ps[bank][:], ("ps", bank), bd_bf[:], "bd", True, True)
        rr = rstd_from_bank(b2, 64.0)
        A("dve", lambda e: e.scalar_tensor_tensor(out=out_ap, in0=ps[bank][:], scalar=cvl(l, gname), in1=tmpf[:, rr, :],
                                                  op0=ALU.mult, op1=ALU.mult),
          r=[("ps", bank), ("tmpf", rr), "cv"], w=[out_buf])

    def conv_chain(n, l):
        for k in range(CONVK):
            for c in range(4):
                if k == 0:
                    A("dve", lambda e, c=c: e.tensor_scalar(out=cc[:, c, :], in0=hbuf[:, c, 0:T], scalar1=cvl(l, "convw", c * CONVK),
                                                            scalar2=cvl(l, "convb", c), op0=ALU.mult, op1=ALU.add),
                      r=[("hbuf", c), "cv"], w=[("cc", c)])
                else:
                    A("dve", lambda e, c=c, k=k: e.scalar_tensor_tensor(out=cc[:, c, :], in0=hbuf[:, c, k:k + T],
                                                                        scalar=cvl(l, "convw", c * CONVK + k), in1=cc[:, c, :],
                                                                        op0=ALU.mult, op1=ALU.add),
                      r=[("hbuf", c), ("cc", c), "cv"], w=[("cc", c)])
            yield
        for c in range(4):
            A("dve", lambda e, c=c: e.tensor_copy(out=hcar[l][:, c, :], in_=hbuf[:, c, T:HB]),
              r=[("hbuf", c)], w=[("hcar", l)])
        bsum = next_bank()
        bsq = next_bank()
        for c in range(4):
            q = nxt("sqb", NSQ)
            A("act", lambda e, c=c, q=q: e.activation(out=sqb[:, q, :], in_=cc[:, c, :], func=AF.Copy),
              r=[("cc", c)], w=[("sqb", q)])
            A("pe", lambda e, c=c, q=q: e.matmul(ps[bsum][:], lhsT=ones_bf[:], rhs=sqb[:, q, :], start=(c == 0), stop=(c == 3)),
              r=[("sqb", q), "ones"], w=[("ps", bsum)])
        for c in range(4):
            sumsq_accum(bsq, cc[:, c, :], ("cc", c), ones_bf[:], "ones", c == 0, c == 3)
        tm = nxt("tmpf", NTF)
        A("act", lambda e: e.activation(out=tmpf[:, tm, :], in_=ps[bsum][:], func=AF.Copy, scale=1.0 / 512.0),
          r=[("ps", bsum)], w=[("tmpf", tm)])
        tq = nxt("tmpf", NTF)
        A("dve", lambda e: e.tensor_tensor(out=tmpf[:, tq, :], in0=tmpf[:, tm, :], in1=tmpf[:, tm, :], op=ALU.mult),
          r=[("tmpf", tm)], w=[("tmpf", tq)])
        A("dve", lambda e: e.scalar_tensor_tensor(out=tmpf[:, tq, :], in0=ps[bsq][:], scalar=1.0 / 512.0, in1=tmpf[:, tq, :],
                                                  op0=ALU.mult, op1=ALU.subtract),
          r=[("ps", bsq), ("tmpf", tq)], w=[("tmpf", tq)])
        ts_ = nxt("tmpf", NTF)
        A("act", lambda e: e.activation(out=tmpf[:, ts_, :], in_=tmpf[:, tq, :], func=AF.Sqrt, bias=eps_ap, scale=1.0),
          r=[("tmpf", tq), "eps"], w=[("tmpf", ts_)])
        tr = nxt("tmpf", NTF)
        A("dve", lambda e: e.reciprocal(out=tmpf[:, tr, :], in_=tmpf[:, ts_, :]), r=[("tmpf", ts_)], w=[("tmpf", tr)])
        for c in range(4):
            A("dve", lambda e, c=c: e.tensor_tensor(out=cc[:, c, :], in0=cc[:, c, :], in1=tmpf[:, tm, :], op=ALU.subtract),
              r=[("cc", c), ("tmpf", tm)], w=[("cc", c)])
            A("dve", lambda e, c=c: e.tensor_tensor(out=cc[:, c, :], in0=cc[:, c, :], in1=tmpf[:, tr, :], op=ALU.mult),
              r=[("cc", c), ("tmpf", tr)], w=[("cc", c)])
            A("act", lambda e, c=c: e.activation(out=cc[:, c, :], in_=cc[:, c, :], func=AF.Silu,
                                                 bias=cvl(l, "clnb", c), scale=cvl(l, "clng", c)),
              r=[("cc", c), "cv"], w=[("cc", c)])
        while not pstate["proj_done"]:
            yield
        group_norm_to_mix(l, cc, "cc", 4, 8)

    def sgu_chain(n, l):
        for c in range(4):
            bank = next_bank()

            def fn(e, c=c, bank=bank):
                ins = None
                for tb in range(NBLK):
                    for hh in range(2):
                        h = 2 * c + hh
                        ins = e.matmul(ps[bank][hh * 64:(hh + 1) * 64, tb * P:(tb + 1) * P],
                                       lhsT=svn[:, tb, h * 64:(h + 1) * 64], rhs=ws_bf[:, l * 8 + h, :],
                                       start=True, stop=True, tile_position=(0, hh * 64))
                return ins
            A("pe", fn, r=[("svn", tb) for tb in range(NBLK)] + ["ws"], w=[("ps", bank)])
            ts = nxt("tmpf", NTF)
            A("dve", lambda e, c=c, bank=bank, ts=ts: e.tensor_tensor(
                out=tmpf[:, ts, :].rearrange("p (b i) -> p b i", i=P), in0=ps[bank][:].rearrange("p (b i) -> p b i", i=P),
                in1=bs_sb[:, l * 4 + c, :].unsqueeze(1).to_broadcast([P, NBLK, P]), op=ALU.add),
              r=[("ps", bank), "bs"], w=[("tmpf", ts)])
            A("dve", lambda e, c=c, ts=ts: e.tensor_tensor(out=u_sb[:, c, :], in0=u_sb[:, c, :], in1=tmpf[:, ts, :], op=ALU.mult),
              r=[("u", c), ("tmpf", ts)], w=[("u", c)])

    def attention(n, l, pump):
        def emit_scores(cq, tp):
            pts = []
            for hq in range(2):
                h = 2 * cq + hq
                g = h // 4
                kv = (g // 2) if (g % 2 == hq) else (2 + g // 2)
                bs_ = next_bank()

                def fn(e, hq=hq, kv=kv, bs_=bs_, tp=tp, cq=cq):
                    ins = None
                    for t2 in range(2):
                        tb = tp * 2 + t2
                        for part in range(2):
                            if part == 0:
                                kop = kcar[l][hq * 64:(hq + 1) * 64, kv, :] if tb == 0 else kn[hq * 64:(hq + 1) * 64, kv, (tb - 1) * P:tb * P]
                            else:
                                kop = kn[hq * 64:(hq + 1) * 64, kv, tb * P:(tb + 1) * P]
                            col = (t2 * 2 + part) * P
                            ins = e.matmul(ps[bs_][:, col:col + P], lhsT=kop,
                                           rhs=qn[hq * 64:(hq + 1) * 64, cq, tb * P:(tb + 1) * P],
                                           start=True, stop=True, tile_position=(hq * 64, 0))
                    return ins
                A("pe", fn, r=[("kn", kv), ("kcar", l), ("qn", cq)], w=[("ps", bs_)])
                pt = nxt("ptb", NPT)
                A("act", lambda e, bs_=bs_, pt=pt: e.activation(out=ptb[:, pt, :], in_=ps[bs_][:], func=AF.Exp, scale=0.125),
                  r=[("ps", bs_)], w=[("ptb", pt)])
                mi = 1 if (n == 0 and tp == 0) else 0
                A("dve", lambda e, pt=pt, mi=mi: e.tensor_tensor(out=ptb[:, pt, :], in0=ptb[:, pt, :], in1=mask_bf[:, mi, :], op=ALU.mult),
                  r=[("ptb", pt), "mask"], w=[("ptb", pt)])
                pts.append(pt)
            return tuple(pts)

        cur = {}

        def emit_pv(cq, tp, pts):
            if tp == 0:
                cur["by"] = next_bank()
                cur["bd"] = next_bank()
            by, bd_ = cur["by"], cur["bd"]

            def fn2(e, tp=tp, pts=pts, by=by, bd_=bd_, cq=cq):
                ins = None
                for hq in range(2):
                    h = 2 * cq + hq
                    g = h // 4
                    for t2 in range(2):
                        tb = tp * 2 + t2
                        for part in range(2):
                            if part == 0:
                                vop = vcar[l][:, g * 64:(g + 1) * 64] if tb == 0 else vt[:, tb - 1, g * 64:(g + 1) * 64]
                            else:
                                vop = vt[:, tb, g * 64:(g + 1) * 64]
                            col = (t2 * 2 + part) * P
                            ins = e.matmul(ps[by][hq * 64:(hq + 1) * 64, tb * P:(tb + 1) * P], lhsT=vop,
                                           rhs=ptb[:, pts[hq], col:col + P], start=(part == 0), stop=(part == 1),
                                           tile_position=(0, hq * 64))
                        for part in range(2):
                            col = (t2 * 2 + part) * P
                            ins = e.matmul(ps[bd_][hq * 64:(hq + 1) * 64, tb * P:(tb + 1) * P], lhsT=ones_bf[:, 0:64],
                                           rhs=ptb[:, pts[hq], col:col + P], start=(part == 0), stop=(part == 1),
                                           tile_position=(0, hq * 64))
                return ins
            A("pe", fn2, r=[("ptb", pts[0]), ("ptb", pts[1]), ("vcar", l), "ones"] + [("vt", b) for b in range(NBLK)],
              w=[("ps", by), ("ps", bd_)])
            if tp == 1:
                ts = nxt("tmpf", NTF)
                A("dve", lambda e, cq=cq, bd_=bd_, ts=ts: e.tensor_scalar(out=tmpf[:, ts, :], in0=ps[bd_][:], scalar1=esink[:, l * 8 + cq:l * 8 + cq + 1],
                                                                         scalar2=None, op0=ALU.add),
                  r=[("ps", bd_), "esink"], w=[("tmpf", ts)])
                A("dve", lambda e, ts=ts: e.reciprocal(out=tmpf[:, ts, :], in_=tmpf[:, ts, :]), r=[("tmpf", ts)], w=[("tmpf", ts)])
                A("dve", lambda e, cq=cq, by=by, ts=ts: e.tensor_tensor(out=ya[:, cq, :], in0=ps[by][:], in1=tmpf[:, ts, :], op=ALU.mult),
                  r=[("ps", by), ("tmpf", ts)], w=[("ya", cq)])
                pump(2)

        pend = None
        for cq in range(8):
            for tp in range(2):
                pts = emit_scores(cq, tp)
                if pend is not None:
                    emit_pv(*pend)
                pend = (cq, tp, pts)
        emit_pv(*pend)
        A("dve", lambda e: e.tensor_copy(out=kcar[l][:], in_=kn[:, :, (NBLK - 1) * P:NBLK * P]),
          r=[("kn", c) for c in range(4)], w=[("kcar", l)])
        A("dve", lambda e: e.tensor_copy(out=vcar[l][:], in_=vt[:, NBLK - 1, :]), r=[("vt", NBLK - 1)], w=[("vcar", l)])

    for n in range(NT):
        for l in range(L):
            step(n, l)
    A("act", None, w=[("x", m) for m in range(KC)])
    if debug:
        for nm in dbg_list:
            A("sp", None, dma=None, r=[], w=[])
        S_.dbg_final = [("dbg_" + nm) for nm in dbg_list]

    S_.finalize()
    semkeys = set(S_.dma_val.keys()) | {e for e in S_.ENGS if S_.counts[e] > 0}
    sems = {k: es.enter_context(nc.semaphore(str(k))) for k in sorted(semkeys)}
    with nc.Block() as block:
        @block.tensor
        def _(e):
            S_.emit_engine("pe", e, sems)

        @block.scalar
        def _(e):
            S_.emit_engine("act", e, sems)

        @block.vector
        def _(e):
            S_.emit_engine("dve", e, sems)

        @block.gpsimd
        def _(e):
            S_.emit_engine("pool", e, sems)

        @block.sync
        def _(e):
            S_.emit_engine("sp", e, sems)
            for k in getattr(S_, "dbg_final", []):
                e.wait_ge(sems[k], S_.dma_val[k])
    es.close()
    nc._dbg_list = dbg_list
    return nc


def host_consts(L, ln1_g, q_norm_g, k_norm_g, sinks, conv_w, conv_b, conv_ln_g, conv_ln_b,
                sgu_ln_g, sgu_ln_b, sgu_w, sgu_b, out_norm_g, ln2_g):
    f = np.float32
    cvec = np.zeros((P, L * NCV), f)
    pidx = np.arange(P)
    for l in range(L):
        b = l * NCV
        cvec[:, b + OFF["g1"]:b + OFF["g1"] + 16] = np.asarray(ln1_g[l], f).reshape(16, P).T
        cvec[:, b + OFF["g2"]:b + OFF["g2"] + 16] = np.asarray(ln2_g[l], f).reshape(16, P).T
        cvec[:, b + OFF["go"]:b + OFF["go"] + 16] = np.asarray(out_norm_g[l], f).reshape(16, P).T
        cvec[:, b + OFF["gq"]] = np.asarray(q_norm_g[l], f)[pidx % 64]
        cvec[:, b + OFF["gk"]] = np.asarray(k_norm_g[l], f)[pidx % 64]
        sk = np.asarray(sinks[l], f)
        for c in range(8):
            cvec[:, b + OFF["sink"] + c] = sk[2 * c + pidx // 64]
        cw = np.asarray(conv_w[l], f)
        for c in range(4):
            cvec[:, b + OFF["convw"] + c * CONVK:b + OFF["convw"] + (c + 1) * CONVK] = cw[:, c * P:(c + 1) * P].T
        cvec[:, b + OFF["convb"]:b + OFF["convb"] + 4] = np.asarray(conv_b[l], f).reshape(4, P).T
        cvec[:, b + OFF["clng"]:b + OFF["clng"] + 4] = np.asarray(conv_ln_g[l], f).reshape(4, P).T
        cvec[:, b + OFF["clnb"]:b + OFF["clnb"] + 4] = np.asarray(conv_ln_b[l], f).reshape(4, P).T
    cbc = np.zeros((P, L, 2, 512), f)
    cbs = np.zeros((P, L, 4, 128), f)
    cws = np.zeros((P, L, 8, 128), f)
    for l in range(L):
        cbc[:, l, 0, :] = np.asarray(sgu_ln_g[l], f)[None, :]
        cbc[:, l, 1, :] = np.asarray(sgu_ln_b[l], f)[None, :]
        sbv = np.asarray(sgu_b[l], f)
        for c in range(4):
            cbs[:, l, c, :] = sbv[2 * c + pidx // 64, :]
        cws[:, l, :, :] = np.transpose(np.asarray(sgu_w[l], f), (2, 0, 1))
    s_ = np.arange(P)[:, None]
    q_ = np.arange(P)[None, :]
    prev = (s_ > q_).astype(f)
    cur = (q_ >= s_).astype(f)
    cmat = np.zeros((P, 1152), f)
    cmat[:, 0:512] = np.concatenate([prev, cur, prev, cur], axis=1)
    cmat[:, 512:1024] = np.concatenate([np.zeros_like(prev), cur, prev, cur], axis=1)
    perm = np.zeros((P, P), f)
    perm[(np.arange(P) + 64) % P, np.arange(P)] = 1.0
    cmat[:, 1024:1152] = perm
    return dict(cvec=cvec, cbc=cbc.reshape(P, -1), cbs=cbs.reshape(P, -1), cws=cws.reshape(P, -1), cmat=cmat)


_NC_CACHE = {}
_LAST_RES = None


def kernel(x, ln1_g, w_in, q_norm_g, k_norm_g, sinks, conv_w, conv_b, conv_ln_g, conv_ln_b,
           sgu_ln_g, sgu_ln_b, sgu_w, sgu_b, out_norm_g, w_out, ln2_g, w_up, w_down):
    x = np.asarray(x, np.float32)
    B, S, _ = x.shape
    L = int(np.asarray(w_in).shape[0])
    key = (S, L)
    if key not in _NC_CACHE:
        _NC_CACHE[key] = build_nc(S, L)
    nc = _NC_CACHE[key]
    consts = host_consts(L, ln1_g, q_norm_g, k_norm_g, sinks, conv_w, conv_b, conv_ln_g, conv_ln_b,
                         sgu_ln_g, sgu_ln_b, sgu_w, sgu_b, out_norm_g, ln2_g)
    shared = dict(w_in=np.ascontiguousarray(w_in, np.float32), w_out=np.ascontiguousarray(w_out, np.float32),
                  w_up=np.ascontiguousarray(w_up, np.float32), w_down=np.ascontiguousarray(w_down, np.float32), **consts)
    in_maps = []
    for b in range(B):
        m = dict(shared)
        m["xT"] = np.ascontiguousarray(x[b].T)
        in_maps.append(m)
    res = run_bass_kernel_spmd(nc, in_maps, core_ids=list(range(B)))
    global _LAST_RES
    _LAST_RES = res
    out = np.stack([np.ascontiguousarray(np.asarray(r["yT"]).T) for r in res.results], axis=0)
    return out.astype(np.float32)
```

```python
import contextlib
import numpy as np
import concourse.bass as bass
import concourse.mybir as mybir
from concourse.bass_utils import run_bass_kernel_spmd

F32 = mybir.dt.float32
BF16 = mybir.dt.bfloat16
AF = mybir.ActivationFunctionType
ALU = mybir.AluOpType

P = 128
D = 2048
KC = D // P
DFF = 8192
FC = DFF // P
DIN = 3584
T = 512
NBLK = T // P
EPS = 1e-6
CONVK = 31
NSLAB = 43
SLAB_ELEMS = 8192
SAME_ENGINE_SYNC = True

OFF = {}
_o = 0
for _nm, _w in (("g1", 16), ("g2", 16), ("go", 16), ("gq", 1), ("gk", 1), ("sink", 8),
                ("convw", 4 * CONVK), ("convb", 4), ("clng", 4), ("clnb", 4)):
    OFF[_nm] = _o
    _o += _w
NCV = _o


class Sched:
    ENGS = ("pe", "act", "dve", "pool", "sp")

    def __init__(self):
        self.ops = []
        self.by_eng = {e: [] for e in self.ENGS}
        self.state = {}
        self.alias = {}
        self.dma_val = {}
        self.dma_last = {}

    def set_alias_groups(self, regions):
        names = list(regions)
        for a in names:
            sa, ea = regions[a]
            self.alias[a] = [b for b in names if regions[b][0] < ea and sa < regions[b][1]]

    def _al(self, b):
        return self.alias.get(b, (b,))

    def op(self, eng, fn, r=(), w=(), dma=None, ndma=1):
        oid = len(self.ops)
        deps = {}
        for b in r:
            for a in self._al(b):
                st = self.state.get(a)
                if st and st[0] is not None:
                    deps[st[0]] = True
        for b in w:
            for a in self._al(b):
                st = self.state.get(a)
                if st:
                    if st[0] is not None:
                        deps.setdefault(st[0], False)
                    for o_ in st[1].values():
                        deps.setdefault(o_, False)
        rec = dict(id=oid, eng=eng, fn=fn, deps=deps, dma=dma, ndma=ndma, target=False)
        if dma is not None:
            if dma in self.dma_last:
                deps[self.dma_last[dma]] = True
            self.dma_last[dma] = oid
            v = self.dma_val.get(dma, 0) + 16 * ndma
            self.dma_val[dma] = v
            rec["sig"] = (dma, v)
        self.ops.append(rec)
        self.by_eng[eng].append(rec)
        sk = dma if dma is not None else eng
        for b in r:
            st = self.state.setdefault(b, [None, {}])
            st[1][sk] = oid
        for b in w:
            self.state[b] = [oid, {}]
        return oid

    def _needs_wait(self, r, dr, raw=True):
        if dr["dma"] is not None:
            return True
        if r["dma"] is None and dr["eng"] == r["eng"]:
            if dr["eng"] == "pe":
                return False
            return SAME_ENGINE_SYNC and raw
        return True

    def finalize(self):
        for r in self.ops:
            for d, raw in r["deps"].items():
                dr = self.ops[d]
                if dr["dma"] is None and self._needs_wait(r, dr, raw):
                    dr["target"] = True
        cnt = {e: 0 for e in self.ENGS}
        for r in self.ops:
            if r["dma"] is None and r["target"]:
                cnt[r["eng"]] += 1
                r["sig"] = (r["eng"], cnt[r["eng"]])
        self.counts = cnt

    def emit_engine(self, ename, eng, sems):
        waited = {}
        for r in self.by_eng[ename]:
            need = {}
            for d, raw in r["deps"].items():
                dr = self.ops[d]
                if not self._needs_wait(r, dr, raw):
                    continue
                sk, v = dr["sig"]
                if need.get(sk, 0) < v:
                    need[sk] = v
            for sk, v in need.items():
                if waited.get(sk, 0) < v:
                    eng.wait_ge(sems[sk], v)
                    waited[sk] = v
            if r["fn"] is None:
                continue
            ins = r["fn"](eng)
            if "sig" in r:
                sk = r["sig"][0]
                if r["dma"] is not None:
                    assert isinstance(ins, (list, tuple)) and len(ins) == r["ndma"]
                    for i_ in ins:
                        i_.then_inc(sems[sk], 16)
                else:
                    ins.then_inc(sems[sk], 1)


def build_nc(S, L, debug=False):
    NT = S // T
    dbg_list = []
    nc = bass.Bass("TRN2", target_bir_lowering=False)
    xT = nc.dram_tensor("xT", [D, S], F32, kind="ExternalInput").ap()
    w_in = nc.dram_tensor("w_in", [L, D, DIN], F32, kind="ExternalInput").ap()
    w_out = nc.dram_tensor("w_out", [L, D, D], F32, kind="ExternalInput").ap()
    w_up = nc.dram_tensor("w_up", [L, D, DFF], F32, kind="ExternalInput").ap()
    w_down = nc.dram_tensor("w_down", [L, DFF, D], F32, kind="ExternalInput").ap()
    cvec = nc.dram_tensor("cvec", [P, L * NCV], F32, kind="ExternalInput").ap()
    cbc = nc.dram_tensor("cbc", [P, L * 2 * 512], F32, kind="ExternalInput").ap()
    cbs = nc.dram_tensor("cbs", [P, L * 4 * 128], F32, kind="ExternalInput").ap()
    cws = nc.dram_tensor("cws", [P, L * 8 * 128], F32, kind="ExternalInput").ap()
    cmat = nc.dram_tensor("cmat", [P, 1152], F32, kind="ExternalInput").ap()
    yT = nc.dram_tensor("yT", [D, S], F32, kind="ExternalOutput").ap()
    scr = nc.dram_tensor("wscr", [L, NSLAB, P, SLAB_ELEMS], BF16, kind="Internal").ap()

    S_ = Sched()
    es = contextlib.ExitStack()

    def sb(name, shape, dt):
        return es.enter_context(nc.sbuf_tensor(name, shape, dt))

    x_sb = sb("x_sb", [P, KC, T], F32)
    act_sb = sb("act_sb", [P, KC, T], BF16)
    slab = [sb(f"slab{i}", [P, SLAB_ELEMS], BF16) for i in range(2)]
    cv_sb = sb("cv_sb", [P, L * NCV], F32)
    gbc_sb = sb("gbc_sb", [P, L * 2, 512], F32)
    bs_sb = sb("bs_sb", [P, L * 4, 128], F32)
    ws_bf = sb("ws_bf", [P, L * 8, 128], BF16)
    mask_bf = sb("mask_bf", [P, 2, 512], BF16)
    perm_bf = sb("perm_bf", [P, P], BF16)
    ones_bf = sb("ones_bf", [P, P], BF16)
    bd_bf = sb("bd_bf", [P, P], BF16)
    esink = sb("esink", [P, L * 8], F32)
    kcar = [sb(f"kcar{l}", [P, 4, P], BF16) for l in range(L)]
    vcar = [sb(f"vcar{l}", [P, 256], BF16) for l in range(L)]
    hcar = [sb(f"hcar{l}", [P, 4, CONVK - 1], F32) for l in range(L)]
    small = sb("small", [P, 64], F32)

    UB = 88064
    U = sb("U", [P, UB // 4], F32)
    regions = {}

    def carve(name, off, nbytes, dt, pattern=None, **kw):
        assert off % 4 == 0 and nbytes % 4 == 0
        ap = U[:, off // 4:(off + nbytes) // 4]
        if dt == BF16:
            ap = ap.bitcast(BF16)
        if pattern:
            ap = ap.rearrange(pattern, **kw)
        return ap

    o = 0
    hid = carve("hid", 0, 65536, BF16, "p (j t) -> p j t", t=T)
    for j in range(FC):
        regions[("hid", j)] = (j * 1024, (j + 1) * 1024)
    qn = carve("qn", o, 8192, BF16, "p (c t) -> p c t", t=T)
    for c in range(8):
        regions[("qn", c)] = (o + c * 1024, o + (c + 1) * 1024)
    o += 8192
    kn = carve("kn", o, 4096, BF16, "p (c t) -> p c t", t=T)
    for c in range(4):
        regions[("kn", c)] = (o + c * 1024, o + (c + 1) * 1024)
    o += 4096
    vt = carve("vt", o, 2048, BF16, "p (b d) -> p b d", d=256)
    for b in range(4):
        regions[("vt", b)] = (o + b * 512, o + (b + 1) * 512)
    o += 2048
    svn = carve("svn", o, 4096, BF16, "p (b d) -> p b d", d=512)
    for b in range(4):
        regions[("svn", b)] = (o + b * 1024, o + (b + 1) * 1024)
    o += 4096
    u_sb = carve("u", o, 8192, F32, "p (c t) -> p c t", t=T)
    for c in range(4):
        regions[("u", c)] = (o + c * 2048, o + (c + 1) * 2048)
    o += 8192
    HB = T + CONVK - 1
    hbuf = carve("hbuf", o, 4 * HB * 4, F32, "p (c t) -> p c t", t=HB)
    for c in range(4):
        regions[("hbuf", c)] = (o + c * HB * 4, o + (c + 1) * HB * 4)
    o += 4 * HB * 4
    cc = carve("cc", o, 8192, F32, "p (c t) -> p c t", t=T)
    for c in range(4):
        regions[("cc", c)] = (o + c * 2048, o + (c + 1) * 2048)
    o += 8192
    ya = carve("ya", o, 16384, F32, "p (c t) -> p c t", t=T)
    for c in range(8):
        regions[("ya", c)] = (o + c * 2048, o + (c + 1) * 2048)
    o += 16384
    assert o <= 65536, o
    stg_ws = carve("stg_ws", 0, L * 8 * 128 * 4, F32)
    regions["stg_ws"] = (0, L * 8 * 128 * 4)
    stg_cm = carve("stg_cm", 16384, 1152 * 4, F32)
    regions["stg_cm"] = (16384, 16384 + 1152 * 4)
    o = 65536
    NTF = 6
    tmpf = carve("tmpf", o, NTF * 2048, F32, "p (s t) -> p s t", t=T)
    for s in range(NTF):
        regions[("tmpf", s)] = (o + s * 2048, o + (s + 1) * 2048)
    o += NTF * 2048
    NSQ = 4
    sqb = carve("sqb", o, NSQ * 1024, BF16, "p (s t) -> p s t", t=T)
    for s in range(NSQ):
        regions[("sqb", s)] = (o + s * 1024, o + (s + 1) * 1024)
    o += NSQ * 1024
    NPT = 6
    ptb = carve("ptb", o, NPT * 1024, BF16, "p (s t) -> p s t", t=T)
    for s in range(NPT):
        regions[("ptb", s)] = (o + s * 1024, o + (s + 1) * 1024)
    o += NPT * 1024
    assert o <= UB, o
    S_.set_alias_groups(regions)

    ps = [es.enter_context(nc.psum_tensor(f"ps{i}", [P, 512], F32)) for i in range(8)]

    ctr = dict(bank=0, tmpf=0, sqb=0, ptb=0, slab=0, small=0)

    def nxt(kind, n):
        v = ctr[kind]
        ctr[kind] = (v + 1) % n
        return v

    RMS_BANK = 7

    def next_bank():
        return nxt("bank", 7)

    def cvl(l, name, i=0):
        c0 = l * NCV + OFF[name] + i
        return cv_sb[:, c0:c0 + 1]

    A = S_.op
    pstate = {}

    def dump(name, ap, shape, dt, bufs):
        if not debug:
            return
        d = nc.dram_tensor("dbg_" + name, list(shape), dt, kind="ExternalOutput").ap()
        dbg_list.append(name)
        A("sp", lambda e: [e.dma_start(out=d, in_=ap)], r=bufs, dma="dbg_" + name)

    A("sp", lambda e: [e.dma_start(out=cv_sb[:], in_=cvec)], w=["cv"], dma="cst")
    A("sp", lambda e: [e.dma_start(out=gbc_sb[:].rearrange("p a b -> p (a b)"), in_=cbc)], w=["gbc"], dma="cst")
    A("sp", lambda e: [e.dma_start(out=bs_sb[:].rearrange("p a b -> p (a b)"), in_=cbs)], w=["bs"], dma="cst")
    A("sp", lambda e: [e.dma_start(out=stg_ws, in_=cws)], w=["stg_ws"], dma="cst")
    A("sp", lambda e: [e.dma_start(out=stg_cm, in_=cmat)], w=["stg_cm"], dma="cst")
    A("dve", lambda e: e.memset(ones_bf[:], 1.0), w=["ones"])
    A("dve", lambda e: e.memset(bd_bf[:], 0.0), w=["bd"])
    A("dve", lambda e: e.memset(bd_bf[0:64, 0:64], 1.0), w=["bd"])
    A("dve", lambda e: e.memset(bd_bf[64:128, 64:128], 1.0), w=["bd"])
    A("dve", lambda e: e.tensor_copy(out=mask_bf[:].rearrange("p a b -> p (a b)"), in_=stg_cm[:, 0:1024]),
      r=["stg_cm"], w=["mask"])
    A("dve", lambda e: e.tensor_copy(out=perm_bf[:], in_=stg_cm[:, 1024:1152]), r=["stg_cm"], w=["perm"])
    for i in range(L * 8):
        A("dve", lambda e, i=i: e.tensor_tensor(out=ws_bf[:, i, :], in0=stg_ws[:, i * 128:(i + 1) * 128],
                                               in1=stg_cm[:, 128:256], op=ALU.mult),
          r=["stg_ws", "stg_cm"], w=["ws"])
    for l in range(L):
        A("act", lambda e, l=l: e.activation(out=esink[:, l * 8:(l + 1) * 8],
                                             in_=cv_sb[:, l * NCV + OFF["sink"]:l * NCV + OFF["sink"] + 8],
                                             func=AF.Exp), r=["cv"], w=["esink"])
        A("dve", lambda e, l=l: e.memset(kcar[l][:], 0.0), w=[("kcar", l)])
        A("dve", lambda e, l=l: e.memset(vcar[l][:], 0.0), w=[("vcar", l)])
        A("dve", lambda e, l=l: e.memset(hcar[l][:], 0.0), w=[("hcar", l)])

    def slab_src(l, s):
        if s < 7:
            return w_in[l].rearrange("(kc p) m -> p kc m", p=P)[:, :, s * 512:(s + 1) * 512], 512
        if s < 11:
            return w_out[l].rearrange("(kc p) m -> p kc m", p=P)[:, :, (s - 7) * 512:(s - 6) * 512], 512
        if s < 27:
            return w_up[l].rearrange("(kc p) m -> p kc m", p=P)[:, :, (s - 11) * 512:(s - 10) * 512], 512
        return w_down[l].rearrange("(kc p) m -> p kc m", p=P)[:, :, (s - 27) * 128:(s - 26) * 128], 128

    W_IN_ORDER = [3, 4, 6, 5, 2, 0, 1]

    def load_slab(l, s, n):
        b = nxt("slab", 2)
        if n == 0:
            src_ap, mw = slab_src(l, s)
            dst = slab[b][:].rearrange("p (kc m) -> p kc m", m=mw)
            A("pool", lambda e, src_ap=src_ap, dst=dst: [e.dma_start(out=dst, in_=src_ap)],
              w=[("slab", b)], dma=f"slab{b}")
            if NT > 1:
                A("sp", lambda e, b=b, l=l, s=s: [e.dma_start(out=scr[l, s], in_=slab[b][:])],
                  r=[("slab", b)], w=[("scr", l, s)], dma=f"st{b}")
        else:
            A("sp", lambda e, b=b, l=l, s=s: [e.dma_start(out=slab[b][:], in_=scr[l, s])],
              r=[("scr", l, s)], w=[("slab", b)], dma=f"slab{b}")
        return b

    def rstd_from_bank(bank, width):
        s1 = nxt("tmpf", NTF)
        A("act", lambda e: e.activation(out=tmpf[:, s1, :], in_=ps[bank][:], func=AF.Ln,
                                        bias=eps_ap, scale=1.0 / width),
          r=[("ps", bank), "eps"], w=[("tmpf", s1)])
        A("act", lambda e: e.activation(out=tmpf[:, s1, :], in_=tmpf[:, s1, :], func=AF.Exp, scale=-0.5),
          r=[("tmpf", s1)], w=[("tmpf", s1)])
        return s1

    def sumsq_accum(bank, src_ap, src_buf, lhsT_ap, lhsT_buf, first, last):
        q = nxt("sqb", NSQ)
        A("act", lambda e: e.activation(out=sqb[:, q, :], in_=src_ap, func=AF.Square),
          r=[src_buf], w=[("sqb", q)])
        A("pe", lambda e: e.matmul(ps[bank][:], lhsT=lhsT_ap, rhs=sqb[:, q, :], start=first, stop=last),
          r=[("sqb", q), lhsT_buf], w=[("ps", bank)])

    def rms_accum(bank, kc):
        sumsq_accum(bank, x_sb[:, kc, :], ("x", kc), ones_bf[:], "ones", kc == 0, kc == KC - 1)

    def rmsnorm_to_act(l, gname, bank=None):
        if bank is None:
            bank = next_bank()
            for kc in range(KC):
                rms_accum(bank, kc)
        rr = rstd_from_bank(bank, float(D))
        for kc in range(KC):
            A("dve", lambda e, kc=kc: e.scalar_tensor_tensor(
                out=act_sb[:, kc, :], in0=x_sb[:, kc, :], scalar=cvl(l, gname, kc), in1=tmpf[:, rr, :],
                op0=ALU.mult, op1=ALU.mult),
              r=[("x", kc), ("tmpf", rr), "cv"], w=[("act", kc)])

    def group_norm_to_mix(l, ybuf, yname, nch, mix_off):
        bank = next_bank()
        for c in range(nch):
            sumsq_accum(bank, ybuf[:, c, :], (yname, c), ones_bf[:], "ones", c == 0, c == nch - 1)
        rr = rstd_from_bank(bank, float(nch * P))
        for c in range(nch):
            A("dve", lambda e, c=c: e.scalar_tensor_tensor(
                out=act_sb[:, mix_off + c, :], in0=ybuf[:, c, :], scalar=cvl(l, "go", mix_off + c),
                in1=tmpf[:, rr, :], op0=ALU.mult, op1=ALU.mult),
              r=[(yname, c), ("tmpf", rr), "cv"], w=[("act", mix_off + c)])

    def fm_group(bank, sbuf_i, kcn, mw, j, rhs_fn):
        sv = slab[sbuf_i][:].rearrange("p (kc m) -> p kc m", m=mw)

        def fn(e):
            ins = None
            for kc in range(kcn):
                ins = e.matmul(ps[bank][:], lhsT=sv[:, kc, j * P:(j + 1) * P], rhs=rhs_fn(kc)[0],
                               start=(kc == 0), stop=(kc == kcn - 1))
            return ins
        A("pe", fn, r=[("slab", sbuf_i)] + [rhs_fn(kc)[1] for kc in range(kcn)], w=[("ps", bank)])

    def act_rhs(kc):
        return act_sb[:, kc, :], ("act", kc)

    def hid_rhs(kc):
        return hid[:, kc, :], ("hid", kc)

    eps_ap = small[:, 0:1]
    A("dve", lambda e: e.memset(small[:, 0:1], EPS), w=["eps"])

    def step(n, l):
        t0 = n * T
        cgh = []
        pstate["proj_done"] = False

        def pump(k=1):
            for _ in range(k):
                if cgh:
                    next(cgh[0], None)
        if l == 0:
            for kc in range(KC):
                A("sp", lambda e, kc=kc: [e.dma_start(out=x_sb[:, kc, :], in_=xT[kc * P:(kc + 1) * P, t0:t0 + T])],
                  w=[("x", kc)], dma=f"xs{kc}")
        for c in range(4):
            A("dve", lambda e, c=c: e.tensor_copy(out=hbuf[:, c, 0:CONVK - 1], in_=hcar[l][:, c, :]),
              r=[("hcar", l)], w=[("hbuf", c)])
        rmsnorm_to_act(l, "g1", bank=(pstate.pop("rms1_bank") if l > 0 else None))

        for s in W_IN_ORDER:
            sbi = load_slab(l, s, n)
            if s == 3:
                for j in range(4):
                    bank = next_bank()
                    fm_group(bank, sbi, KC, 512, j, act_rhs)
                    A("act", lambda e, j=j, bank=bank: e.activation(out=hbuf[:, j, CONVK - 1:HB], in_=ps[bank][:], func=AF.Copy),
                      r=[("ps", bank)], w=[("hbuf", j)])
            elif s == 4:
                for j in range(4):
                    bank = next_bank()
                    fm_group(bank, sbi, KC, 512, j, act_rhs)
                    ts = nxt("tmpf", NTF)
                    A("act", lambda e, bank=bank, ts=ts: e.activation(out=tmpf[:, ts, :], in_=ps[bank][:], func=AF.Sigmoid),
                      r=[("ps", bank)], w=[("tmpf", ts)])
                    A("dve", lambda e, j=j, ts=ts: e.tensor_tensor(out=hbuf[:, j, CONVK - 1:HB], in0=hbuf[:, j, CONVK - 1:HB],
                                                                   in1=tmpf[:, ts, :], op=ALU.mult),
                      r=[("tmpf", ts), ("hbuf", j)], w=[("hbuf", j)])
                cgh.append(conv_chain(n, l))
            elif s == 6:
                sv = slab[sbi][:].rearrange("p (kc m) -> p kc m", m=512)
                for tb in range(NBLK):
                    bank = next_bank()

                    def fn(e, tb=tb, bank=bank, sv=sv):
                        ins = None
                        for kc in range(KC):
                            ins = e.matmul(ps[bank][:], lhsT=act_sb[:, kc, tb * P:(tb + 1) * P], rhs=sv[:, kc, :],
                                           start=(kc == 0), stop=(kc == KC - 1))
                        return ins
                    A("pe", fn, r=[("slab", sbi)] + [("act", kc) for kc in range(KC)], w=[("ps", bank)])
                    sm = nxt("small", 4)
                    st_ap = small[:, 8 + sm * 12:8 + sm * 12 + 6]
                    mv_ap = small[:, 8 + sm * 12 + 6:8 + sm * 12 + 8]
                    sd_ap = small[:, 8 + sm * 12 + 8:8 + sm * 12 + 9]
                    rs_ap = small[:, 8 + sm * 12 + 9:8 + sm * 12 + 10]
                    A("dve", lambda e, bank=bank, st_ap=st_ap: e.bn_stats(out=st_ap, in_=ps[bank][:]),
                      r=[("ps", bank)], w=[("sm", sm)])
                    A("dve", lambda e, st_ap=st_ap, mv_ap=mv_ap: e.bn_aggr(out=mv_ap, in_=st_ap), r=[("sm", sm)], w=[("sm", sm)])
                    A("act", lambda e, mv_ap=mv_ap, sd_ap=sd_ap: e.activation(out=sd_ap, in_=mv_ap[:, 1:2], func=AF.Ln, bias=eps_ap, scale=1.0),
                      r=[("sm", sm), "eps"], w=[("sm", sm)])
                    A("act", lambda e, sd_ap=sd_ap, rs_ap=rs_ap: e.activation(out=rs_ap, in_=sd_ap, func=AF.Exp, scale=-0.5), r=[("sm", sm)], w=[("sm", sm)])
                    ts = nxt("tmpf", NTF)
                    A("dve", lambda e, bank=bank, ts=ts, mv_ap=mv_ap, rs_ap=rs_ap: e.tensor_scalar(
                        out=tmpf[:, ts, :], in0=ps[bank][:], scalar1=mv_ap[:, 0:1], scalar2=rs_ap,
                        op0=ALU.subtract, op1=ALU.mult),
                      r=[("ps", bank), ("sm", sm)], w=[("tmpf", ts)])
                    A("dve", lambda e, ts=ts: e.tensor_tensor(out=tmpf[:, ts, :], in0=tmpf[:, ts, :], in1=gbc_sb[:, l * 2, :], op=ALU.mult),
                      r=[("tmpf", ts), "gbc"], w=[("tmpf", ts)])
                    A("dve", lambda e, ts=ts, tb=tb: e.tensor_tensor(out=svn[:, tb, :], in0=tmpf[:, ts, :], in1=gbc_sb[:, l * 2 + 1, :], op=ALU.add),
                      r=[("tmpf", ts), "gbc"], w=[("svn", tb)])
                    pump()
            elif s == 5:
                for j in range(4):
                    bank = next_bank()
                    fm_group(bank, sbi, KC, 512, j, act_rhs)
                    A("act", lambda e, j=j, bank=bank: e.activation(out=u_sb[:, j, :], in_=ps[bank][:], func=AF.Copy),
                      r=[("ps", bank)], w=[("u", j)])
                    pump()
                sgu_chain(n, l)
            elif s == 2:
                for j in range(2):
                    bank = next_bank()
                    fm_group(bank, sbi, KC, 512, j, act_rhs)
                    head_norm(l, bank, "gk", kn[:, j, :], ("kn", j))
                    b2 = next_bank()
                    A("pe", lambda e, j=j, b2=b2: e.matmul(ps[b2][:], lhsT=perm_bf[:], rhs=kn[:, j, :], start=True, stop=True),
                      r=[("kn", j), "perm"], w=[("ps", b2)])
                    A("act", lambda e, j=j, b2=b2: e.activation(out=kn[:, 2 + j, :], in_=ps[b2][:], func=AF.Copy),
                      r=[("ps", b2)], w=[("kn", 2 + j)])
                    pump()
                sv = slab[sbi][:].rearrange("p (kc m) -> p kc m", m=512)
                for tb in range(NBLK):
                    bank = next_bank()

                    def fn(e, tb=tb, bank=bank, sv=sv):
                        ins = None
                        for kc in range(KC):
                            ins = e.matmul(ps[bank][:, 0:256], lhsT=act_sb[:, kc, tb * P:(tb + 1) * P], rhs=sv[:, kc, 256:512],
                                           start=(kc == 0), stop=(kc == KC - 1))
                        return ins
                    A("pe", fn, r=[("slab", sbi)] + [("act", kc) for kc in range(KC)], w=[("ps", bank)])
                    A("act", lambda e, tb=tb, bank=bank: e.activation(out=vt[:, tb, :], in_=ps[bank][:, 0:256], func=AF.Copy),
                      r=[("ps", bank)], w=[("vt", tb)])
                    pump()
            else:
                for j in range(4):
                    bank = next_bank()
                    fm_group(bank, sbi, KC, 512, j, act_rhs)
                    cq = s * 4 + j
                    head_norm(l, bank, "gq", qn[:, cq, :], ("qn", cq))
                    pump()
        pstate["proj_done"] = True
        group_norm_to_mix(l, u_sb, "u", 4, 12)
        attention(n, l, pump)
        pump(1000)
        if n == 0 and l == 0:
            dump("qn", qn, [P, 8, T], BF16, [("qn", c) for c in range(8)])
            dump("kn", kn, [P, 4, T], BF16, [("kn", c) for c in range(4)])
            dump("vt", vt, [P, 4, 256], BF16, [("vt", c) for c in range(4)])
            dump("ya", ya, [P, 8, T], F32, [("ya", c) for c in range(8)])
            dump("svn", svn, [P, 4, 512], BF16, [("svn", c) for c in range(4)])
        group_norm_to_mix(l, ya, "ya", 8, 0)
        if n == 0 and l == 0:
            dump("mix", act_sb[:], [P, KC, T], BF16, [("act", c) for c in range(KC)])

        rb2 = RMS_BANK
        for s in range(7, 11):
            sbi = load_slab(l, s, n)
            for j in range(4):
                m = (s - 7) * 4 + j
                bank = next_bank()
                fm_group(bank, sbi, KC, 512, j, act_rhs)
                if m > 0:
                    rms_accum(rb2, m - 1)
                A("dve", lambda e, m=m, bank=bank: e.tensor_tensor(out=x_sb[:, m, :], in0=ps[bank][:], in1=x_sb[:, m, :], op=ALU.add),
                  r=[("ps", bank), ("x", m)], w=[("x", m)])
        rms_accum(rb2, KC - 1)
        if n == 0 and l == 0:
            dump("x1", x_sb[:], [P, KC, T], F32, [("x", c) for c in range(KC)])
        rmsnorm_to_act(l, "g2", bank=rb2)
        for s in range(11, 27):
            sbi = load_slab(l, s, n)
            for j in range(4):
                f = (s - 11) * 4 + j
                bank = next_bank()
                fm_group(bank, sbi, KC, 512, j, act_rhs)
                ts = nxt("tmpf", NTF)
                A("act", lambda e, bank=bank, ts=ts: e.activation(out=tmpf[:, ts, :], in_=ps[bank][:], func=AF.Relu),
                  r=[("ps", bank)], w=[("tmpf", ts)])
                A("dve", lambda e, bank=bank, ts=ts, f=f: e.tensor_tensor(out=hid[:, f, :], in0=ps[bank][:], in1=tmpf[:, ts, :], op=ALU.mult),
                  r=[("ps", bank), ("tmpf", ts)], w=[("hid", f)])
        if l < L - 1:
            pstate["rms1_bank"] = RMS_BANK
        for s in range(27, 43):
            sbi = load_slab(l, s, n)
            m = s - 27
            bank = next_bank()
            fm_group(bank, sbi, FC, 128, 0, hid_rhs)
            if l < L - 1 and m > 0:
                rms_accum(pstate["rms1_bank"], m - 1)
            A("dve", lambda e, m=m, bank=bank: e.tensor_tensor(out=x_sb[:, m, :], in0=ps[bank][:], in1=x_sb[:, m, :], op=ALU.add),
              r=[("ps", bank), ("x", m)], w=[("x", m)])
            if l == L - 1:
                A("act", lambda e, m=m: [e.dma_start(out=yT[m * P:(m + 1) * P, t0:t0 + T], in_=x_sb[:, m, :])],
                  r=[("x", m)], dma=f"xs{m}")
            elif m == KC - 1:
                rms_accum(pstate["rms1_bank"], m)

    def head_norm(l, bank, gname, out_ap, out_buf):
        b2 = next_bank()
        sumsq_accum(b2, ps[bank][:], ("ps", bank), bd_bf[:], "bd", True, True)
        rr = rstd_from_bank(b2, 64.0)
        A("dve", lambda e: e.scalar_tensor_tensor(out=out_ap, in0=ps[bank][:], scalar=cvl(l, gname), in1=tmpf[:, rr, :],
                                                  op0=ALU.mult, op1=ALU.mult),
          r=[("ps", bank), ("tmpf", rr), "cv"], w=[out_buf])

    def conv_chain(n, l):
        for k in range(CONVK):
            for c in range(4):
                if k == 0:
                    A("dve", lambda e, c=c: e.tensor_scalar(out=cc[:, c, :], in0=hbuf[:, c, 0:T], scalar1=cvl(l, "convw", c * CONVK),
                                                            scalar2=cvl(l, "convb", c), op0=ALU.mult, op1=ALU.add),
                      r=[("hbuf", c), "cv"], w=[("cc", c)])
                else:
                    A("dve", lambda e, c=c, k=k: e.scalar_tensor_tensor(out=cc[:, c, :], in0=hbuf[:, c, k:k + T],
                                                                        scalar=cvl(l, "convw", c * CONVK + k), in1=cc[:, c, :],
                                                                        op0=ALU.mult, op1=ALU.add),
                      r=[("hbuf", c), ("cc", c), "cv"], w=[("cc", c)])
            yield
        for c in range(4):
            A("dve", lambda e, c=c: e.tensor_copy(out=hcar[l][:, c, :], in_=hbuf[:, c, T:HB]),
              r=[("hbuf", c)], w=[("hcar", l)])
        bsum = next_bank()
        bsq = next_bank()
        for c in range(4):
            q = nxt("sqb", NSQ)
            A("act", lambda e, c=c, q=q: e.activation(out=sqb[:, q, :], in_=cc[:, c, :], func=AF.Copy),
              r=[("cc", c)], w=[("sqb", q)])
            A("pe", lambda e, c=c, q=q: e.matmul(ps[bsum][:], lhsT=ones_bf[:], rhs=sqb[:, q, :], start=(c == 0), stop=(c == 3)),
              r=[("sqb", q), "ones"], w=[("ps", bsum)])
        for c in range(4):
            sumsq_accum(bsq, cc[:, c, :], ("cc", c), ones_bf[:], "ones", c == 0, c == 3)
        tm = nxt("tmpf", NTF)
        A("act", lambda e: e.activation(out=tmpf[:, tm, :], in_=ps[bsum][:], func=AF.Copy, scale=1.0 / 512.0),
          r=[("ps", bsum)], w=[("tmpf", tm)])
        tq = nxt("tmpf", NTF)
        A("dve", lambda e: e.tensor_tensor(out=tmpf[:, tq, :], in0=tmpf[:, tm, :], in1=tmpf[:, tm, :], op=ALU.mult),
          r=[("tmpf", tm)], w=[("tmpf", tq)])
        A("dve", lambda e: e.scalar_tensor_tensor(out=tmpf[:, tq, :], in0=ps[bsq][:], scalar=1.0 / 512.0, in1=tmpf[:, tq, :],
                                                  op0=ALU.mult, op1=ALU.subtract),
          r=[("ps", bsq), ("tmpf", tq)], w=[("tmpf", tq)])
        ts_ = nxt("tmpf", NTF)
        A("act", lambda e: e.activation(out=tmpf[:, ts_, :], in_=tmpf[:, tq, :], func=AF.Ln, bias=eps_ap, scale=1.0),
          r=[("tmpf", tq), "eps"], w=[("tmpf", ts_)])
        tr = nxt("tmpf", NTF)
        A("act", lambda e: e.activation(out=tmpf[:, tr, :], in_=tmpf[:, ts_, :], func=AF.Exp, scale=-0.5), r=[("tmpf", ts_)], w=[("tmpf", tr)])
        for c in range(4):
            A("dve", lambda e, c=c: e.tensor_tensor(out=cc[:, c, :], in0=cc[:, c, :], in1=tmpf[:, tm, :], op=ALU.subtract),
              r=[("cc", c), ("tmpf", tm)], w=[("cc", c)])
            A("dve", lambda e, c=c: e.tensor_tensor(out=cc[:, c, :], in0=cc[:, c, :], in1=tmpf[:, tr, :], op=ALU.mult),
              r=[("cc", c), ("tmpf", tr)], w=[("cc", c)])
            A("act", lambda e, c=c: e.activation(out=cc[:, c, :], in_=cc[:, c, :], func=AF.Silu,
                                                 bias=cvl(l, "clnb", c), scale=cvl(l, "clng", c)),
              r=[("cc", c), "cv"], w=[("cc", c)])
        while not pstate["proj_done"]:
            yield
        group_norm_to_mix(l, cc, "cc", 4, 8)

    def sgu_chain(n, l):
        for c in range(4):
            bank = next_bank()

            def fn(e, c=c, bank=bank):
                ins = None
                for tb in range(NBLK):
                    for hh in range(2):
                        h = 2 * c + hh
                        ins = e.matmul(ps[bank][hh * 64:(hh + 1) * 64, tb * P:(tb + 1) * P],
                                       lhsT=svn[:, tb, h * 64:(h + 1) * 64], rhs=ws_bf[:, l * 8 + h, :],
                                       start=True, stop=True, tile_position=(0, hh * 64))
                return ins
            A("pe", fn, r=[("svn", tb) for tb in range(NBLK)] + ["ws"], w=[("ps", bank)])
            ts = nxt("tmpf", NTF)
            A("dve", lambda e, c=c, bank=bank, ts=ts: e.tensor_tensor(
                out=tmpf[:, ts, :].rearrange("p (b i) -> p b i", i=P), in0=ps[bank][:].rearrange("p (b i) -> p b i", i=P),
                in1=bs_sb[:, l * 4 + c, :].unsqueeze(1).to_broadcast([P, NBLK, P]), op=ALU.add),
              r=[("ps", bank), "bs"], w=[("tmpf", ts)])
            A("dve", lambda e, c=c, ts=ts: e.tensor_tensor(out=u_sb[:, c, :], in0=u_sb[:, c, :], in1=tmpf[:, ts, :], op=ALU.mult),
              r=[("u", c), ("tmpf", ts)], w=[("u", c)])

    def attention(n, l, pump):
        def emit_scores(cq, tp):
            pts = []
            for hq in range(2):
                h = 2 * cq + hq
                g = h // 4
                kv = (g // 2) if (g % 2 == hq) else (2 + g // 2)
                bs_ = next_bank()

                def fn(e, hq=hq, kv=kv, bs_=bs_, tp=tp, cq=cq):
                    ins = None
                    for t2 in range(2):
                        tb = tp * 2 + t2
                        for part in range(2):
                            if part == 0:
                                kop = kcar[l][hq * 64:(hq + 1) * 64, kv, :] if tb == 0 else kn[hq * 64:(hq + 1) * 64, kv, (tb - 1) * P:tb * P]
                            else:
                                kop = kn[hq * 64:(hq + 1) * 64, kv, tb * P:(tb + 1) * P]
                            col = (t2 * 2 + part) * P
                            ins = e.matmul(ps[bs_][:, col:col + P], lhsT=kop,
                                           rhs=qn[hq * 64:(hq + 1) * 64, cq, tb * P:(tb + 1) * P],
                                           start=True, stop=True, tile_position=(hq * 64, 0))
                    return ins
                A("pe", fn, r=[("kn", kv), ("kcar", l), ("qn", cq)], w=[("ps", bs_)])
                pt = nxt("ptb", NPT)
                A("act", lambda e, bs_=bs_, pt=pt: e.activation(out=ptb[:, pt, :], in_=ps[bs_][:], func=AF.Exp, scale=0.125),
                  r=[("ps", bs_)], w=[("ptb", pt)])
                mi = 1 if (n == 0 and tp == 0) else 0
                A("dve", lambda e, pt=pt, mi=mi: e.tensor_tensor(out=ptb[:, pt, :], in0=ptb[:, pt, :], in1=mask_bf[:, mi, :], op=ALU.mult),
                  r=[("ptb", pt), "mask"], w=[("ptb", pt)])
                pts.append(pt)
            return tuple(pts)

        cur = {}

        def emit_pv(cq, tp, pts):
            if tp == 0:
                cur["by"] = next_bank()
                cur["bd"] = next_bank()
            by, bd_ = cur["by"], cur["bd"]

            def fn2(e, tp=tp, pts=pts, by=by, bd_=bd_, cq=cq):
                ins = None
                for hq in range(2):
                    h = 2 * cq + hq
                    g = h // 4
                    for t2 in range(2):
                        tb = tp * 2 + t2
                        for part in range(2):
                            if part == 0:
                                vop = vcar[l][:, g * 64:(g + 1) * 64] if tb == 0 else vt[:, tb - 1, g * 64:(g + 1) * 64]
                            else:
                                vop = vt[:, tb, g * 64:(g + 1) * 64]
                            col = (t2 * 2 + part) * P
                            ins = e.matmul(ps[by][hq * 64:(hq + 1) * 64, tb * P:(tb + 1) * P], lhsT=vop,
                                           rhs=ptb[:, pts[hq], col:col + P], start=(part == 0), stop=(part == 1),
                                           tile_position=(0, hq * 64))
                        for part in range(2):
                            col = (t2 * 2 + part) * P
                            ins = e.matmul(ps[bd_][hq * 64:(hq + 1) * 64, tb * P:(tb + 1) * P], lhsT=ones_bf[:, 0:64],
                                           rhs=ptb[:, pts[hq], col:col + P], start=(part == 0), stop=(part == 1),
                                           tile_position=(0, hq * 64))
                return ins
            A("pe", fn2, r=[("ptb", pts[0]), ("ptb", pts[1]), ("vcar", l), "ones"] + [("vt", b) for b in range(NBLK)],
              w=[("ps", by), ("ps", bd_)])
            if tp == 1:
                ts = nxt("tmpf", NTF)
                A("act", lambda e, cq=cq, bd_=bd_, ts=ts: e.activation(out=tmpf[:, ts, :], in_=ps[bd_][:], func=AF.Ln,
                                                                      bias=esink[:, l * 8 + cq:l * 8 + cq + 1], scale=1.0),
                  r=[("ps", bd_), "esink"], w=[("tmpf", ts)])
                A("act", lambda e, ts=ts: e.activation(out=tmpf[:, ts, :], in_=tmpf[:, ts, :], func=AF.Exp, scale=-1.0),
                  r=[("tmpf", ts)], w=[("tmpf", ts)])
                A("dve", lambda e, cq=cq, by=by, ts=ts: e.tensor_tensor(out=ya[:, cq, :], in0=ps[by][:], in1=tmpf[:, ts, :], op=ALU.mult),
                  r=[("ps", by), ("tmpf", ts)], w=[("ya", cq)])
                pump(2)

        pend = None
        for cq in range(8):
            for tp in range(2):
                pts = emit_scores(cq, tp)
                if pend is not None:
                    emit_pv(*pend)
                pend = (cq, tp, pts)
        emit_pv(*pend)
        A("dve", lambda e: e.tensor_copy(out=kcar[l][:], in_=kn[:, :, (NBLK - 1) * P:NBLK * P]),
          r=[("kn", c) for c in range(4)], w=[("kcar", l)])
        A("dve", lambda e: e.tensor_copy(out=vcar[l][:], in_=vt[:, NBLK - 1, :]), r=[("vt", NBLK - 1)], w=[("vcar", l)])

    for n in range(NT):
        for l in range(L):
            step(n, l)
    A("act", None, w=[("x", m) for m in range(KC)])
    if debug:
        for nm in dbg_list:
            A("sp", None, dma=None, r=[], w=[])
        S_.dbg_final = [("dbg_" + nm) for nm in dbg_list]

    S_.finalize()
    semkeys = set(S_.dma_val.keys()) | {e for e in S_.ENGS if S_.counts[e] > 0}
    sems = {k: es.enter_context(nc.semaphore(str(k))) for k in sorted(semkeys)}
    with nc.Block() as block:
        @block.tensor
        def _(e):
            S_.emit_engine("pe", e, sems)

        @block.scalar
        def _(e):
            S_.emit_engine("act", e, sems)

        @block.vector
        def _(e):
            S_.emit_engine("dve", e, sems)

        @block.gpsimd
        def _(e):
            S_.emit_engine("pool", e, sems)

        @block.sync
        def _(e):
            S_.emit_engine("sp", e, sems)
            for k in getattr(S_, "dbg_final", []):
                e.wait_ge(sems[k], S_.dma_val[k])
    es.close()
    nc._dbg_list = dbg_list
    return nc


def host_consts(L, ln1_g, q_norm_g, k_norm_g, sinks, conv_w, conv_b, conv_ln_g, conv_ln_b,
                sgu_ln_g, sgu_ln_b, sgu_w, sgu_b, out_norm_g, ln2_g):
    f = np.float32
    cvec = np.zeros((P, L * NCV), f)
    pidx = np.arange(P)
    for l in range(L):
        b = l * NCV
        cvec[:, b + OFF["g1"]:b + OFF["g1"] + 16] = np.asarray(ln1_g[l], f).reshape(16, P).T
        cvec[:, b + OFF["g2"]:b + OFF["g2"] + 16] = np.asarray(ln2_g[l], f).reshape(16, P).T
        cvec[:, b + OFF["go"]:b + OFF["go"] + 16] = np.asarray(out_norm_g[l], f).reshape(16, P).T
        cvec[:, b + OFF["gq"]] = np.asarray(q_norm_g[l], f)[pidx % 64]
        cvec[:, b + OFF["gk"]] = np.asarray(k_norm_g[l], f)[pidx % 64]
        sk = np.asarray(sinks[l], f)
        for c in range(8):
            cvec[:, b + OFF["sink"] + c] = sk[2 * c + pidx // 64]
        cw = np.asarray(conv_w[l], f)
        for c in range(4):
            cvec[:, b + OFF["convw"] + c * CONVK:b + OFF["convw"] + (c + 1) * CONVK] = cw[:, c * P:(c + 1) * P].T
        cvec[:, b + OFF["convb"]:b + OFF["convb"] + 4] = np.asarray(conv_b[l], f).reshape(4, P).T
        cvec[:, b + OFF["clng"]:b + OFF["clng"] + 4] = np.asarray(conv_ln_g[l], f).reshape(4, P).T
        cvec[:, b + OFF["clnb"]:b + OFF["clnb"] + 4] = np.asarray(conv_ln_b[l], f).reshape(4, P).T
    cbc = np.zeros((P, L, 2, 512), f)
    cbs = np.zeros((P, L, 4, 128), f)
    cws = np.zeros((P, L, 8, 128), f)
    for l in range(L):
        cbc[:, l, 0, :] = np.asarray(sgu_ln_g[l], f)[None, :]
        cbc[:, l, 1, :] = np.asarray(sgu_ln_b[l], f)[None, :]
        sbv = np.asarray(sgu_b[l], f)
        for c in range(4):
            cbs[:, l, c, :] = sbv[2 * c + pidx // 64, :]
        cws[:, l, :, :] = np.transpose(np.asarray(sgu_w[l], f), (2, 0, 1))
    s_ = np.arange(P)[:, None]
    q_ = np.arange(P)[None, :]
    prev = (s_ > q_).astype(f)
    cur = (q_ >= s_).astype(f)
    cmat = np.zeros((P, 1152), f)
    cmat[:, 0:512] = np.concatenate([prev, cur, prev, cur], axis=1)
    cmat[:, 512:1024] = np.concatenate([np.zeros_like(prev), cur, prev, cur], axis=1)
    perm = np.zeros((P, P), f)
    perm[(np.arange(P) + 64) % P, np.arange(P)] = 1.0
    cmat[:, 1024:1152] = perm
    return dict(cvec=cvec, cbc=cbc.reshape(P, -1), cbs=cbs.reshape(P, -1), cws=cws.reshape(P, -1), cmat=cmat)


_NC_CACHE = {}
_LAST_RES = None


def kernel(x, ln1_g, w_in, q_norm_g, k_norm_g, sinks, conv_w, conv_b, conv_ln_g, conv_ln_b,
           sgu_ln_g, sgu_ln_b, sgu_w, sgu_b, out_norm_g, w_out, ln2_g, w_up, w_down):
    x = np.asarray(x, np.float32)
    B, S, _ = x.shape
    L = int(np.asarray(w_in).shape[0])
    key = (S, L)
    if key not in _NC_CACHE:
        _NC_CACHE[key] = build_nc(S, L)
    nc = _NC_CACHE[key]
    consts = host_consts(L, ln1_g, q_norm_g, k_norm_g, sinks, conv_w, conv_b, conv_ln_g, conv_ln_b,
                         sgu_ln_g, sgu_ln_b, sgu_w, sgu_b, out_norm_g, ln2_g)
    shared = dict(w_in=np.ascontiguousarray(w_in, np.float32), w_out=np.ascontiguousarray(w_out, np.float32),
                  w_up=np.ascontiguousarray(w_up, np.float32), w_down=np.ascontiguousarray(w_down, np.float32), **consts)
    in_maps = []
    for b in range(B):
        m = dict(shared)
        m["xT"] = np.ascontiguousarray(x[b].T)
        in_maps.append(m)
    res = run_bass_kernel_spmd(nc, in_maps, core_ids=list(range(B)))
    global _LAST_RES
    _LAST_RES = res
    out = np.stack([np.ascontiguousarray(np.asarray(r["yT"]).T) for r in res.results], axis=0)
    return out.astype(np.float32)
```

```python
import contextlib
import numpy as np
import concourse.bass as bass
import concourse.mybir as mybir
from concourse.bass_utils import run_bass_kernel_spmd

F32 = mybir.dt.float32
BF16 = mybir.dt.bfloat16
AF = mybir.ActivationFunctionType
ALU = mybir.AluOpType

P = 128
D = 2048
KC = D // P
DFF = 8192
FC = DFF // P
DIN = 3584
T = 512
NBLK = T // P
EPS = 1e-6
CONVK = 31
NSLAB = 43
SLAB_ELEMS = 8192
SAME_ENGINE_SYNC = True

OFF = {}
_o = 0
for _nm, _w in (("g1", 16), ("g2", 16), ("go", 16), ("gq", 1), ("gk", 1), ("sink", 8),
                ("convw", 4 * CONVK), ("convb", 4), ("clng", 4), ("clnb", 4)):
    OFF[_nm] = _o
    _o += _w
NCV = _o


class Sched:
    ENGS = ("pe", "act", "dve", "pool", "sp")

    def __init__(self):
        self.ops = []
        self.by_eng = {e: [] for e in self.ENGS}
        self.state = {}
        self.alias = {}
        self.dma_val = {}
        self.dma_last = {}

    def set_alias_groups(self, regions):
        names = list(regions)
        for a in names:
            sa, ea = regions[a]
            self.alias[a] = [b for b in names if regions[b][0] < ea and sa < regions[b][1]]

    def _al(self, b):
        return self.alias.get(b, (b,))

    def op(self, eng, fn, r=(), w=(), dma=None, ndma=1):
        oid = len(self.ops)
        deps = {}
        for b in r:
            for a in self._al(b):
                st = self.state.get(a)
                if st and st[0] is not None:
                    deps[st[0]] = True
        for b in w:
            for a in self._al(b):
                st = self.state.get(a)
                if st:
                    if st[0] is not None:
                        deps.setdefault(st[0], False)
                    for o_ in st[1].values():
                        deps.setdefault(o_, False)
        rec = dict(id=oid, eng=eng, fn=fn, deps=deps, dma=dma, ndma=ndma, target=False)
        if dma is not None:
            if dma in self.dma_last:
                deps[self.dma_last[dma]] = True
            self.dma_last[dma] = oid
            v = self.dma_val.get(dma, 0) + 16 * ndma
            self.dma_val[dma] = v
            rec["sig"] = (dma, v)
        self.ops.append(rec)
        self.by_eng[eng].append(rec)
        sk = dma if dma is not None else eng
        for b in r:
            st = self.state.setdefault(b, [None, {}])
            st[1][sk] = oid
        for b in w:
            self.state[b] = [oid, {}]
        return oid

    def _needs_wait(self, r, dr, raw=True):
        if dr["dma"] is not None:
            return True
        if r["dma"] is None and dr["eng"] == r["eng"]:
            if dr["eng"] == "pe":
                return False
            return SAME_ENGINE_SYNC and raw
        return True

    def finalize(self):
        for r in self.ops:
            for d, raw in r["deps"].items():
                dr = self.ops[d]
                if dr["dma"] is None and self._needs_wait(r, dr, raw):
                    dr["target"] = True
        cnt = {e: 0 for e in self.ENGS}
        for r in self.ops:
            if r["dma"] is None and r["target"]:
                cnt[r["eng"]] += 1
                r["sig"] = (r["eng"], cnt[r["eng"]])
        self.counts = cnt

    def emit_engine(self, ename, eng, sems):
        waited = {}
        for r in self.by_eng[ename]:
            need = {}
            for d, raw in r["deps"].items():
                dr = self.ops[d]
                if not self._needs_wait(r, dr, raw):
                    continue
                sk, v = dr["sig"]
                if need.get(sk, 0) < v:
                    need[sk] = v
            for sk, v in need.items():
                if waited.get(sk, 0) < v:
                    eng.wait_ge(sems[sk], v)
                    waited[sk] = v
            if r["fn"] is None:
                continue
            ins = r["fn"](eng)
            if "sig" in r:
                sk = r["sig"][0]
                if r["dma"] is not None:
                    assert isinstance(ins, (list, tuple)) and len(ins) == r["ndma"]
                    for i_ in ins:
                        i_.then_inc(sems[sk], 16)
                else:
                    ins.then_inc(sems[sk], 1)


def build_nc(S, L, debug=False):
    NT = S // T
    dbg_list = []
    nc = bass.Bass("TRN2", target_bir_lowering=False)
    xT = nc.dram_tensor("xT", [D, S], F32, kind="ExternalInput").ap()
    w_in = nc.dram_tensor("w_in", [L, D, DIN], F32, kind="ExternalInput").ap()
    w_out = nc.dram_tensor("w_out", [L, D, D], F32, kind="ExternalInput").ap()
    w_up = nc.dram_tensor("w_up", [L, D, DFF], F32, kind="ExternalInput").ap()
    w_down = nc.dram_tensor("w_down", [L, DFF, D], F32, kind="ExternalInput").ap()
    cvec = nc.dram_tensor("cvec", [P, L * NCV], F32, kind="ExternalInput").ap()
    cbc = nc.dram_tensor("cbc", [P, L * 2 * 512], F32, kind="ExternalInput").ap()
    cbs = nc.dram_tensor("cbs", [P, L * 4 * 128], F32, kind="ExternalInput").ap()
    cws = nc.dram_tensor("cws", [P, L * 8 * 128], F32, kind="ExternalInput").ap()
    cmat = nc.dram_tensor("cmat", [P, 1152], F32, kind="ExternalInput").ap()
    yT = nc.dram_tensor("yT", [D, S], F32, kind="ExternalOutput").ap()
    scr = nc.dram_tensor("wscr", [L, NSLAB, P, SLAB_ELEMS], BF16, kind="Internal").ap()

    S_ = Sched()
    es = contextlib.ExitStack()

    def sb(name, shape, dt):
        return es.enter_context(nc.sbuf_tensor(name, shape, dt))

    x_sb = sb("x_sb", [P, KC, T], F32)
    act_sb = sb("act_sb", [P, KC, T], BF16)
    NSLB = 3
    slab = [sb(f"slab{i}", [P, SLAB_ELEMS], BF16) for i in range(NSLB)]
    cv_sb = sb("cv_sb", [P, L * NCV], F32)
    gbc_sb = sb("gbc_sb", [P, L * 2, 512], F32)
    bs_sb = sb("bs_sb", [P, L * 4, 128], F32)
    ws_bf = sb("ws_bf", [P, L * 8, 128], BF16)
    mask_bf = sb("mask_bf", [P, 2, 512], BF16)
    perm_bf = sb("perm_bf", [P, P], BF16)
    ones_bf = sb("ones_bf", [P, P], BF16)
    bd_bf = sb("bd_bf", [P, P], BF16)
    esink = sb("esink", [P, L * 8], F32)
    kcar = [sb(f"kcar{l}", [P, 4, P], BF16) for l in range(L)]
    vcar = [sb(f"vcar{l}", [P, 256], BF16) for l in range(L)]
    hcar = [sb(f"hcar{l}", [P, 4, CONVK - 1], F32) for l in range(L)]
    small = sb("small", [P, 64], F32)

    UB = 88064
    U = sb("U", [P, UB // 4], F32)
    regions = {}

    def carve(name, off, nbytes, dt, pattern=None, **kw):
        assert off % 4 == 0 and nbytes % 4 == 0
        ap = U[:, off // 4:(off + nbytes) // 4]
        if dt == BF16:
            ap = ap.bitcast(BF16)
        if pattern:
            ap = ap.rearrange(pattern, **kw)
        return ap

    o = 0
    hid = carve("hid", 0, 65536, BF16, "p (j t) -> p j t", t=T)
    for j in range(FC):
        regions[("hid", j)] = (j * 1024, (j + 1) * 1024)
    qn = carve("qn", o, 8192, BF16, "p (c t) -> p c t", t=T)
    for c in range(8):
        regions[("qn", c)] = (o + c * 1024, o + (c + 1) * 1024)
    o += 8192
    kn = carve("kn", o, 4096, BF16, "p (c t) -> p c t", t=T)
    for c in range(4):
        regions[("kn", c)] = (o + c * 1024, o + (c + 1) * 1024)
    o += 4096
    vt = carve("vt", o, 2048, BF16, "p (b d) -> p b d", d=256)
    for b in range(4):
        regions[("vt", b)] = (o + b * 512, o + (b + 1) * 512)
    o += 2048
    svn = carve("svn", o, 4096, BF16, "p (b d) -> p b d", d=512)
    for b in range(4):
        regions[("svn", b)] = (o + b * 1024, o + (b + 1) * 1024)
    o += 4096
    u_sb = carve("u", o, 8192, F32, "p (c t) -> p c t", t=T)
    for c in range(4):
        regions[("u", c)] = (o + c * 2048, o + (c + 1) * 2048)
    o += 8192
    HB = T + CONVK - 1
    hbuf = carve("hbuf", o, 4 * HB * 4, F32, "p (c t) -> p c t", t=HB)
    for c in range(4):
        regions[("hbuf", c)] = (o + c * HB * 4, o + (c + 1) * HB * 4)
    o += 4 * HB * 4
    cc = carve("cc", o, 8192, F32, "p (c t) -> p c t", t=T)
    for c in range(4):
        regions[("cc", c)] = (o + c * 2048, o + (c + 1) * 2048)
    o += 8192
    ya = carve("ya", o, 16384, F32, "p (c t) -> p c t", t=T)
    for c in range(8):
        regions[("ya", c)] = (o + c * 2048, o + (c + 1) * 2048)
    o += 16384
    assert o <= 65536, o
    stg_ws = carve("stg_ws", 0, L * 8 * 128 * 4, F32)
    regions["stg_ws"] = (0, L * 8 * 128 * 4)
    stg_cm = carve("stg_cm", 16384, 1152 * 4, F32)
    regions["stg_cm"] = (16384, 16384 + 1152 * 4)
    o = 65536
    NTF = 6
    tmpf = carve("tmpf", o, NTF * 2048, F32, "p (s t) -> p s t", t=T)
    for s in range(NTF):
        regions[("tmpf", s)] = (o + s * 2048, o + (s + 1) * 2048)
    o += NTF * 2048
    NSQ = 4
    sqb = carve("sqb", o, NSQ * 1024, BF16, "p (s t) -> p s t", t=T)
    for s in range(NSQ):
        regions[("sqb", s)] = (o + s * 1024, o + (s + 1) * 1024)
    o += NSQ * 1024
    NPT = 6
    ptb = carve("ptb", o, NPT * 1024, BF16, "p (s t) -> p s t", t=T)
    for s in range(NPT):
        regions[("ptb", s)] = (o + s * 1024, o + (s + 1) * 1024)
    o += NPT * 1024
    assert o <= UB, o
    S_.set_alias_groups(regions)

    ps = [es.enter_context(nc.psum_tensor(f"ps{i}", [P, 512], F32)) for i in range(8)]

    ctr = dict(bank=0, tmpf=0, sqb=0, ptb=0, slab=0, small=0)

    def nxt(kind, n):
        v = ctr[kind]
        ctr[kind] = (v + 1) % n
        return v

    RMS_BANK = 7

    def next_bank():
        return nxt("bank", 7)

    def cvl(l, name, i=0):
        c0 = l * NCV + OFF[name] + i
        return cv_sb[:, c0:c0 + 1]

    A = S_.op
    pstate = {}

    def dump(name, ap, shape, dt, bufs):
        if not debug:
            return
        d = nc.dram_tensor("dbg_" + name, list(shape), dt, kind="ExternalOutput").ap()
        dbg_list.append(name)
        A("sp", lambda e: [e.dma_start(out=d, in_=ap)], r=bufs, dma="dbg_" + name)

    A("sp", lambda e: [e.dma_start(out=cv_sb[:], in_=cvec)], w=["cv"], dma="cst")
    A("sp", lambda e: [e.dma_start(out=gbc_sb[:].rearrange("p a b -> p (a b)"), in_=cbc)], w=["gbc"], dma="cst")
    A("sp", lambda e: [e.dma_start(out=bs_sb[:].rearrange("p a b -> p (a b)"), in_=cbs)], w=["bs"], dma="cst")
    A("sp", lambda e: [e.dma_start(out=stg_ws, in_=cws)], w=["stg_ws"], dma="cst")
    A("sp", lambda e: [e.dma_start(out=stg_cm, in_=cmat)], w=["stg_cm"], dma="cst")
    A("dve", lambda e: e.memset(ones_bf[:], 1.0), w=["ones"])
    A("dve", lambda e: e.memset(bd_bf[:], 0.0), w=["bd"])
    A("dve", lambda e: e.memset(bd_bf[0:64, 0:64], 1.0), w=["bd"])
    A("dve", lambda e: e.memset(bd_bf[64:128, 64:128], 1.0), w=["bd"])
    A("dve", lambda e: e.tensor_copy(out=mask_bf[:].rearrange("p a b -> p (a b)"), in_=stg_cm[:, 0:1024]),
      r=["stg_cm"], w=["mask"])
    A("dve", lambda e: e.tensor_copy(out=perm_bf[:], in_=stg_cm[:, 1024:1152]), r=["stg_cm"], w=["perm"])
    for i in range(L * 8):
        A("dve", lambda e, i=i: e.tensor_tensor(out=ws_bf[:, i, :], in0=stg_ws[:, i * 128:(i + 1) * 128],
                                               in1=stg_cm[:, 128:256], op=ALU.mult),
          r=["stg_ws", "stg_cm"], w=["ws"])
    for l in range(L):
        A("act", lambda e, l=l: e.activation(out=esink[:, l * 8:(l + 1) * 8],
                                             in_=cv_sb[:, l * NCV + OFF["sink"]:l * NCV + OFF["sink"] + 8],
                                             func=AF.Exp), r=["cv"], w=["esink"])
        A("dve", lambda e, l=l: e.memset(kcar[l][:], 0.0), w=[("kcar", l)])
        A("dve", lambda e, l=l: e.memset(vcar[l][:], 0.0), w=[("vcar", l)])
        A("dve", lambda e, l=l: e.memset(hcar[l][:], 0.0), w=[("hcar", l)])

    def slab_src(l, s):
        if s < 7:
            return w_in[l].rearrange("(kc p) m -> p kc m", p=P)[:, :, s * 512:(s + 1) * 512], 512
        if s < 11:
            return w_out[l].rearrange("(kc p) m -> p kc m", p=P)[:, :, (s - 7) * 512:(s - 6) * 512], 512
        if s < 27:
            return w_up[l].rearrange("(kc p) m -> p kc m", p=P)[:, :, (s - 11) * 512:(s - 10) * 512], 512
        mg, kq = divmod(s - 27, 4)
        return w_down[l].rearrange("(kc p) m -> p kc m", p=P)[:, kq * KC:(kq + 1) * KC, mg * 512:(mg + 1) * 512], 512

    W_IN_ORDER = [3, 4, 6, 5, 2, 0, 1]

    def load_slab(l, s, n):
        b = nxt("slab", NSLB)
        if n == 0:
            src_ap, mw = slab_src(l, s)
            dst = slab[b][:].rearrange("p (kc m) -> p kc m", m=mw)
            A("pool", lambda e, src_ap=src_ap, dst=dst: [e.dma_start(out=dst, in_=src_ap)],
              w=[("slab", b)], dma=f"slab{b}")
            if NT > 1:
                A("sp", lambda e, b=b, l=l, s=s: [e.dma_start(out=scr[l, s], in_=slab[b][:])],
                  r=[("slab", b)], w=[("scr", l, s)], dma=f"st{b}")
        else:
            A("sp", lambda e, b=b, l=l, s=s: [e.dma_start(out=slab[b][:], in_=scr[l, s])],
              r=[("scr", l, s)], w=[("slab", b)], dma=f"slab{b}")
        return b

    def rstd_from_bank(bank, width):
        s1 = nxt("tmpf", NTF)
        A("act", lambda e: e.activation(out=tmpf[:, s1, :], in_=ps[bank][:], func=AF.Ln,
                                        bias=eps_ap, scale=1.0 / width),
          r=[("ps", bank), "eps"], w=[("tmpf", s1)])
        A("act", lambda e: e.activation(out=tmpf[:, s1, :], in_=tmpf[:, s1, :], func=AF.Exp, scale=-0.5),
          r=[("tmpf", s1)], w=[("tmpf", s1)])
        return s1

    def sumsq_accum(bank, src_ap, src_buf, lhsT_ap, lhsT_buf, first, last):
        q = nxt("sqb", NSQ)
        A("act", lambda e: e.activation(out=sqb[:, q, :], in_=src_ap, func=AF.Square),
          r=[src_buf], w=[("sqb", q)])
        A("pe", lambda e: e.matmul(ps[bank][:], lhsT=lhsT_ap, rhs=sqb[:, q, :], start=first, stop=last),
          r=[("sqb", q), lhsT_buf], w=[("ps", bank)])

    def rms_accum(bank, kc):
        sumsq_accum(bank, x_sb[:, kc, :], ("x", kc), ones_bf[:], "ones", kc == 0, kc == KC - 1)

    def rmsnorm_to_act(l, gname, bank=None):
        if bank is None:
            bank = next_bank()
            for kc in range(KC):
                rms_accum(bank, kc)
        rr = rstd_from_bank(bank, float(D))
        for kc in range(KC):
            A("dve", lambda e, kc=kc: e.scalar_tensor_tensor(
                out=act_sb[:, kc, :], in0=x_sb[:, kc, :], scalar=cvl(l, gname, kc), in1=tmpf[:, rr, :],
                op0=ALU.mult, op1=ALU.mult),
              r=[("x", kc), ("tmpf", rr), "cv"], w=[("act", kc)])

    def group_norm_to_mix(l, ybuf, yname, nch, mix_off):
        bank = next_bank()
        for c in range(nch):
            sumsq_accum(bank, ybuf[:, c, :], (yname, c), ones_bf[:], "ones", c == 0, c == nch - 1)
        rr = rstd_from_bank(bank, float(nch * P))
        for c in range(nch):
            A("dve", lambda e, c=c: e.scalar_tensor_tensor(
                out=act_sb[:, mix_off + c, :], in0=ybuf[:, c, :], scalar=cvl(l, "go", mix_off + c),
                in1=tmpf[:, rr, :], op0=ALU.mult, op1=ALU.mult),
              r=[(yname, c), ("tmpf", rr), "cv"], w=[("act", mix_off + c)])

    def fm_group(bank, sbuf_i, kcn, mw, j, rhs_fn):
        sv = slab[sbuf_i][:].rearrange("p (kc m) -> p kc m", m=mw)

        def fn(e):
            ins = None
            for kc in range(kcn):
                ins = e.matmul(ps[bank][:], lhsT=sv[:, kc, j * P:(j + 1) * P], rhs=rhs_fn(kc)[0],
                               start=(kc == 0), stop=(kc == kcn - 1))
            return ins
        A("pe", fn, r=[("slab", sbuf_i)] + [rhs_fn(kc)[1] for kc in range(kcn)], w=[("ps", bank)])

    def act_rhs(kc):
        return act_sb[:, kc, :], ("act", kc)

    def hid_rhs(kc):
        return hid[:, kc, :], ("hid", kc)

    eps_ap = small[:, 0:1]
    A("dve", lambda e: e.memset(small[:, 0:1], EPS), w=["eps"])

    def step(n, l):
        t0 = n * T
        cgh = []
        pstate["proj_done"] = False

        def pump(k=1):
            for _ in range(k):
                if cgh:
                    next(cgh[0], None)
        if l == 0:
            for kc in range(KC):
                A("sp", lambda e, kc=kc: [e.dma_start(out=x_sb[:, kc, :], in_=xT[kc * P:(kc + 1) * P, t0:t0 + T])],
                  w=[("x", kc)], dma=f"xs{kc}")
        for c in range(4):
            A("dve", lambda e, c=c: e.tensor_copy(out=hbuf[:, c, 0:CONVK - 1], in_=hcar[l][:, c, :]),
              r=[("hcar", l)], w=[("hbuf", c)])
        rmsnorm_to_act(l, "g1", bank=(pstate.pop("rms1_bank") if l > 0 else None))

        for s in W_IN_ORDER:
            sbi = load_slab(l, s, n)
            if s == 3:
                for j in range(4):
                    bank = next_bank()
                    fm_group(bank, sbi, KC, 512, j, act_rhs)
                    A("act", lambda e, j=j, bank=bank: e.activation(out=hbuf[:, j, CONVK - 1:HB], in_=ps[bank][:], func=AF.Copy),
                      r=[("ps", bank)], w=[("hbuf", j)])
            elif s == 4:
                for j in range(4):
                    bank = next_bank()
                    fm_group(bank, sbi, KC, 512, j, act_rhs)
                    ts = nxt("tmpf", NTF)
                    A("act", lambda e, bank=bank, ts=ts: e.activation(out=tmpf[:, ts, :], in_=ps[bank][:], func=AF.Sigmoid),
                      r=[("ps", bank)], w=[("tmpf", ts)])
                    A("dve", lambda e, j=j, ts=ts: e.tensor_tensor(out=hbuf[:, j, CONVK - 1:HB], in0=hbuf[:, j, CONVK - 1:HB],
                                                                   in1=tmpf[:, ts, :], op=ALU.mult),
                      r=[("tmpf", ts), ("hbuf", j)], w=[("hbuf", j)])
                cgh.append(conv_chain(n, l))
            elif s == 6:
                sv = slab[sbi][:].rearrange("p (kc m) -> p kc m", m=512)
                for tb in range(NBLK):
                    bank = next_bank()

                    def fn(e, tb=tb, bank=bank, sv=sv):
                        ins = None
                        for kc in range(KC):
                            ins = e.matmul(ps[bank][:], lhsT=act_sb[:, kc, tb * P:(tb + 1) * P], rhs=sv[:, kc, :],
                                           start=(kc == 0), stop=(kc == KC - 1))
                        return ins
                    A("pe", fn, r=[("slab", sbi)] + [("act", kc) for kc in range(KC)], w=[("ps", bank)])
                    sm = nxt("small", 4)
                    st_ap = small[:, 8 + sm * 12:8 + sm * 12 + 6]
                    mv_ap = small[:, 8 + sm * 12 + 6:8 + sm * 12 + 8]
                    sd_ap = small[:, 8 + sm * 12 + 8:8 + sm * 12 + 9]
                    rs_ap = small[:, 8 + sm * 12 + 9:8 + sm * 12 + 10]
                    A("dve", lambda e, bank=bank, st_ap=st_ap: e.bn_stats(out=st_ap, in_=ps[bank][:]),
                      r=[("ps", bank)], w=[("sm", sm)])
                    A("dve", lambda e, st_ap=st_ap, mv_ap=mv_ap: e.bn_aggr(out=mv_ap, in_=st_ap), r=[("sm", sm)], w=[("sm", sm)])
                    A("act", lambda e, mv_ap=mv_ap, sd_ap=sd_ap: e.activation(out=sd_ap, in_=mv_ap[:, 1:2], func=AF.Ln, bias=eps_ap, scale=1.0),
                      r=[("sm", sm), "eps"], w=[("sm", sm)])
                    A("act", lambda e, sd_ap=sd_ap, rs_ap=rs_ap: e.activation(out=rs_ap, in_=sd_ap, func=AF.Exp, scale=-0.5), r=[("sm", sm)], w=[("sm", sm)])
                    ts = nxt("tmpf", NTF)
                    A("dve", lambda e, bank=bank, ts=ts, mv_ap=mv_ap, rs_ap=rs_ap: e.tensor_scalar(
                        out=tmpf[:, ts, :], in0=ps[bank][:], scalar1=mv_ap[:, 0:1], scalar2=rs_ap,
                        op0=ALU.subtract, op1=ALU.mult),
                      r=[("ps", bank), ("sm", sm)], w=[("tmpf", ts)])
                    A("dve", lambda e, ts=ts: e.tensor_tensor(out=tmpf[:, ts, :], in0=tmpf[:, ts, :], in1=gbc_sb[:, l * 2, :], op=ALU.mult),
                      r=[("tmpf", ts), "gbc"], w=[("tmpf", ts)])
                    A("dve", lambda e, ts=ts, tb=tb: e.tensor_tensor(out=svn[:, tb, :], in0=tmpf[:, ts, :], in1=gbc_sb[:, l * 2 + 1, :], op=ALU.add),
                      r=[("tmpf", ts), "gbc"], w=[("svn", tb)])
                    pump()
            elif s == 5:
                for j in range(4):
                    bank = next_bank()
                    fm_group(bank, sbi, KC, 512, j, act_rhs)
                    A("act", lambda e, j=j, bank=bank: e.activation(out=u_sb[:, j, :], in_=ps[bank][:], func=AF.Copy),
                      r=[("ps", bank)], w=[("u", j)])
                    pump()
                sgu_chain(n, l)
            elif s == 2:
                for j in range(2):
                    bank = next_bank()
                    fm_group(bank, sbi, KC, 512, j, act_rhs)
                    head_norm(l, bank, "gk", kn[:, j, :], ("kn", j))
                    b2 = next_bank()
                    A("pe", lambda e, j=j, b2=b2: e.matmul(ps[b2][:], lhsT=perm_bf[:], rhs=kn[:, j, :], start=True, stop=True),
                      r=[("kn", j), "perm"], w=[("ps", b2)])
                    A("act", lambda e, j=j, b2=b2: e.activation(out=kn[:, 2 + j, :], in_=ps[b2][:], func=AF.Copy),
                      r=[("ps", b2)], w=[("kn", 2 + j)])
                    pump()
                sv = slab[sbi][:].rearrange("p (kc m) -> p kc m", m=512)
                for tb in range(NBLK):
                    bank = next_bank()

                    def fn(e, tb=tb, bank=bank, sv=sv):
                        ins = None
                        for kc in range(KC):
                            ins = e.matmul(ps[bank][:, 0:256], lhsT=act_sb[:, kc, tb * P:(tb + 1) * P], rhs=sv[:, kc, 256:512],
                                           start=(kc == 0), stop=(kc == KC - 1))
                        return ins
                    A("pe", fn, r=[("slab", sbi)] + [("act", kc) for kc in range(KC)], w=[("ps", bank)])
                    A("act", lambda e, tb=tb, bank=bank: e.activation(out=vt[:, tb, :], in_=ps[bank][:, 0:256], func=AF.Copy),
                      r=[("ps", bank)], w=[("vt", tb)])
                    pump()
            else:
                for j in range(4):
                    bank = next_bank()
                    fm_group(bank, sbi, KC, 512, j, act_rhs)
                    cq = s * 4 + j
                    head_norm(l, bank, "gq", qn[:, cq, :], ("qn", cq))
                    pump()
        pstate["proj_done"] = True
        group_norm_to_mix(l, u_sb, "u", 4, 12)
        attention(n, l, pump)
        pump(1000)
        if n == 0 and l == 0:
            dump("qn", qn, [P, 8, T], BF16, [("qn", c) for c in range(8)])
            dump("kn", kn, [P, 4, T], BF16, [("kn", c) for c in range(4)])
            dump("vt", vt, [P, 4, 256], BF16, [("vt", c) for c in range(4)])
            dump("ya", ya, [P, 8, T], F32, [("ya", c) for c in range(8)])
            dump("svn", svn, [P, 4, 512], BF16, [("svn", c) for c in range(4)])
        group_norm_to_mix(l, ya, "ya", 8, 0)
        if n == 0 and l == 0:
            dump("mix", act_sb[:], [P, KC, T], BF16, [("act", c) for c in range(KC)])

        rb2 = RMS_BANK
        for s in range(7, 11):
            sbi = load_slab(l, s, n)
            for j in range(4):
                m = (s - 7) * 4 + j
                bank = next_bank()
                fm_group(bank, sbi, KC, 512, j, act_rhs)
                if m > 0:
                    rms_accum(rb2, m - 1)
                A("dve", lambda e, m=m, bank=bank: e.tensor_tensor(out=x_sb[:, m, :], in0=ps[bank][:], in1=x_sb[:, m, :], op=ALU.add),
                  r=[("ps", bank), ("x", m)], w=[("x", m)])
        rms_accum(rb2, KC - 1)
        if n == 0 and l == 0:
            dump("x1", x_sb[:], [P, KC, T], F32, [("x", c) for c in range(KC)])
        rmsnorm_to_act(l, "g2", bank=rb2)
        for s in range(11, 27):
            sbi = load_slab(l, s, n)
            for j in range(4):
                f = (s - 11) * 4 + j
                bank = next_bank()
                fm_group(bank, sbi, KC, 512, j, act_rhs)
                ts = nxt("tmpf", NTF)
                A("act", lambda e, bank=bank, ts=ts: e.activation(out=tmpf[:, ts, :], in_=ps[bank][:], func=AF.Relu),
                  r=[("ps", bank)], w=[("tmpf", ts)])
                A("dve", lambda e, bank=bank, ts=ts, f=f: e.tensor_tensor(out=hid[:, f, :], in0=ps[bank][:], in1=tmpf[:, ts, :], op=ALU.mult),
                  r=[("ps", bank), ("tmpf", ts)], w=[("hid", f)])
        if l < L - 1:
            pstate["rms1_bank"] = RMS_BANK
        pend_m = []
        for mg in range(4):
            banks = [next_bank() for _ in range(4)]
            for kq in range(4):
                s = 27 + mg * 4 + kq
                sbi = load_slab(l, s, n)
                sv = slab[sbi][:].rearrange("p (kc m) -> p kc m", m=512)
                for j in range(4):
                    def fn(e, sv=sv, j=j, kq=kq, bank=banks[j]):
                        ins = None
                        for kc in range(KC):
                            ins = e.matmul(ps[bank][:], lhsT=sv[:, kc, j * P:(j + 1) * P], rhs=hid[:, kq * KC + kc, :],
                                           start=(kq == 0 and kc == 0), stop=(kq == 3 and kc == KC - 1))
                        return ins
                    A("pe", fn, r=[("slab", sbi)] + [("hid", kq * KC + kc) for kc in range(KC)], w=[("ps", banks[j])])
                    if l < L - 1 and pend_m:
                        rms_accum(pstate["rms1_bank"], pend_m.pop(0))
            for j in range(4):
                m = mg * 4 + j
                A("dve", lambda e, m=m, bank=banks[j]: e.tensor_tensor(out=x_sb[:, m, :], in0=ps[bank][:], in1=x_sb[:, m, :], op=ALU.add),
                  r=[("ps", banks[j]), ("x", m)], w=[("x", m)])
                if l == L - 1:
                    A("act", lambda e, m=m: [e.dma_start(out=yT[m * P:(m + 1) * P, t0:t0 + T], in_=x_sb[:, m, :])],
                      r=[("x", m)], dma=f"xs{m}")
                else:
                    pend_m.append(m)
        while pend_m:
            rms_accum(pstate["rms1_bank"], pend_m.pop(0))

    def head_norm(l, bank, gname, out_ap, out_buf):
        b2 = next_bank()
        sumsq_accum(b2, ps[bank][:], ("ps", bank), bd_bf[:], "bd", True, True)
        rr = rstd_from_bank(b2, 64.0)
        A("dve", lambda e: e.scalar_tensor_tensor(out=out_ap, in0=ps[bank][:], scalar=cvl(l, gname), in1=tmpf[:, rr, :],
                                                  op0=ALU.mult, op1=ALU.mult),
          r=[("ps", bank), ("tmpf", rr), "cv"], w=[out_buf])

    def conv_chain(n, l):
        for k in range(CONVK):
            for c in range(4):
                if k == 0:
                    A("dve", lambda e, c=c: e.tensor_scalar(out=cc[:, c, :], in0=hbuf[:, c, 0:T], scalar1=cvl(l, "convw", c * CONVK),
                                                            scalar2=cvl(l, "convb", c), op0=ALU.mult, op1=ALU.add),
                      r=[("hbuf", c), "cv"], w=[("cc", c)])
                else:
                    A("dve", lambda e, c=c, k=k: e.scalar_tensor_tensor(out=cc[:, c, :], in0=hbuf[:, c, k:k + T],
                                                                        scalar=cvl(l, "convw", c * CONVK + k), in1=cc[:, c, :],
                                                                        op0=ALU.mult, op1=ALU.add),
                      r=[("hbuf", c), ("cc", c), "cv"], w=[("cc", c)])
            yield
        for c in range(4):
            A("dve", lambda e, c=c: e.tensor_copy(out=hcar[l][:, c, :], in_=hbuf[:, c, T:HB]),
              r=[("hbuf", c)], w=[("hcar", l)])
        bsum = next_bank()
        bsq = next_bank()
        for c in range(4):
            q = nxt("sqb", NSQ)
            A("act", lambda e, c=c, q=q: e.activation(out=sqb[:, q, :], in_=cc[:, c, :], func=AF.Copy),
              r=[("cc", c)], w=[("sqb", q)])
            A("pe", lambda e, c=c, q=q: e.matmul(ps[bsum][:], lhsT=ones_bf[:], rhs=sqb[:, q, :], start=(c == 0), stop=(c == 3)),
              r=[("sqb", q), "ones"], w=[("ps", bsum)])
        for c in range(4):
            sumsq_accum(bsq, cc[:, c, :], ("cc", c), ones_bf[:], "ones", c == 0, c == 3)
        tm = nxt("tmpf", NTF)
        A("act", lambda e: e.activation(out=tmpf[:, tm, :], in_=ps[bsum][:], func=AF.Copy, scale=1.0 / 512.0),
          r=[("ps", bsum)], w=[("tmpf", tm)])
        tq = nxt("tmpf", NTF)
        A("dve", lambda e: e.tensor_tensor(out=tmpf[:, tq, :], in0=tmpf[:, tm, :], in1=tmpf[:, tm, :], op=ALU.mult),
          r=[("tmpf", tm)], w=[("tmpf", tq)])
        A("dve", lambda e: e.scalar_tensor_tensor(out=tmpf[:, tq, :], in0=ps[bsq][:], scalar=1.0 / 512.0, in1=tmpf[:, tq, :],
                                                  op0=ALU.mult, op1=ALU.subtract),
          r=[("ps", bsq), ("tmpf", tq)], w=[("tmpf", tq)])
        ts_ = nxt("tmpf", NTF)
        A("act", lambda e: e.activation(out=tmpf[:, ts_, :], in_=tmpf[:, tq, :], func=AF.Ln, bias=eps_ap, scale=1.0),
          r=[("tmpf", tq), "eps"], w=[("tmpf", ts_)])
        tr = nxt("tmpf", NTF)
        A("act", lambda e: e.activation(out=tmpf[:, tr, :], in_=tmpf[:, ts_, :], func=AF.Exp, scale=-0.5), r=[("tmpf", ts_)], w=[("tmpf", tr)])
        for c in range(4):
            A("dve", lambda e, c=c: e.tensor_tensor(out=cc[:, c, :], in0=cc[:, c, :], in1=tmpf[:, tm, :], op=ALU.subtract),
              r=[("cc", c), ("tmpf", tm)], w=[("cc", c)])
            A("dve", lambda e, c=c: e.tensor_tensor(out=cc[:, c, :], in0=cc[:, c, :], in1=tmpf[:, tr, :], op=ALU.mult),
              r=[("cc", c), ("tmpf", tr)], w=[("cc", c)])
            A("act", lambda e, c=c: e.activation(out=cc[:, c, :], in_=cc[:, c, :], func=AF.Silu,
                                                 bias=cvl(l, "clnb", c), scale=cvl(l, "clng", c)),
              r=[("cc", c), "cv"], w=[("cc", c)])
        while not pstate["proj_done"]:
            yield
        group_norm_to_mix(l, cc, "cc", 4, 8)

    def sgu_chain(n, l):
        for c in range(4):
            bank = next_bank()

            def fn(e, c=c, bank=bank):
                ins = None
                for tb in range(NBLK):
                    for hh in range(2):
                        h = 2 * c + hh
                        ins = e.matmul(ps[bank][hh * 64:(hh + 1) * 64, tb * P:(tb + 1) * P],
                                       lhsT=svn[:, tb, h * 64:(h + 1) * 64], rhs=ws_bf[:, l * 8 + h, :],
                                       start=True, stop=True, tile_position=(0, hh * 64))
                return ins
            A("pe", fn, r=[("svn", tb) for tb in range(NBLK)] + ["ws"], w=[("ps", bank)])
            ts = nxt("tmpf", NTF)
            A("dve", lambda e, c=c, bank=bank, ts=ts: e.tensor_tensor(
                out=tmpf[:, ts, :].rearrange("p (b i) -> p b i", i=P), in0=ps[bank][:].rearrange("p (b i) -> p b i", i=P),
                in1=bs_sb[:, l * 4 + c, :].unsqueeze(1).to_broadcast([P, NBLK, P]), op=ALU.add),
              r=[("ps", bank), "bs"], w=[("tmpf", ts)])
            A("dve", lambda e, c=c, ts=ts: e.tensor_tensor(out=u_sb[:, c, :], in0=u_sb[:, c, :], in1=tmpf[:, ts, :], op=ALU.mult),
              r=[("u", c), ("tmpf", ts)], w=[("u", c)])

    def attention(n, l, pump):
        def emit_scores(cq, tp):
            pts = []
            for hq in range(2):
                h = 2 * cq + hq
                g = h // 4
                kv = (g // 2) if (g % 2 == hq) else (2 + g // 2)
                bs_ = next_bank()

                def fn(e, hq=hq, kv=kv, bs_=bs_, tp=tp, cq=cq):
                    ins = None
                    for t2 in range(2):
                        tb = tp * 2 + t2
                        for part in range(2):
                            if part == 0:
                                kop = kcar[l][hq * 64:(hq + 1) * 64, kv, :] if tb == 0 else kn[hq * 64:(hq + 1) * 64, kv, (tb - 1) * P:tb * P]
                            else:
                                kop = kn[hq * 64:(hq + 1) * 64, kv, tb * P:(tb + 1) * P]
                            col = (t2 * 2 + part) * P
                            ins = e.matmul(ps[bs_][:, col:col + P], lhsT=kop,
                                           rhs=qn[hq * 64:(hq + 1) * 64, cq, tb * P:(tb + 1) * P],
                                           start=True, stop=True, tile_position=(hq * 64, 0))
                    return ins
                A("pe", fn, r=[("kn", kv), ("kcar", l), ("qn", cq)], w=[("ps", bs_)])
                pt = nxt("ptb", NPT)
                A("act", lambda e, bs_=bs_, pt=pt: e.activation(out=ptb[:, pt, :], in_=ps[bs_][:], func=AF.Exp, scale=0.125),
                  r=[("ps", bs_)], w=[("ptb", pt)])
                mi = 1 if (n == 0 and tp == 0) else 0
                A("dve", lambda e, pt=pt, mi=mi: e.tensor_tensor(out=ptb[:, pt, :], in0=ptb[:, pt, :], in1=mask_bf[:, mi, :], op=ALU.mult),
                  r=[("ptb", pt), "mask"], w=[("ptb", pt)])
                pts.append(pt)
            return tuple(pts)

        cur = {}

        def emit_pv(cq, tp, pts):
            if tp == 0:
                cur["by"] = next_bank()
                cur["bd"] = next_bank()
            by, bd_ = cur["by"], cur["bd"]

            def fn2(e, tp=tp, pts=pts, by=by, bd_=bd_, cq=cq):
                ins = None
                for hq in range(2):
                    h = 2 * cq + hq
                    g = h // 4
                    for t2 in range(2):
                        tb = tp * 2 + t2
                        for part in range(2):
                            if part == 0:
                                vop = vcar[l][:, g * 64:(g + 1) * 64] if tb == 0 else vt[:, tb - 1, g * 64:(g + 1) * 64]
                            else:
                                vop = vt[:, tb, g * 64:(g + 1) * 64]
                            col = (t2 * 2 + part) * P
                            ins = e.matmul(ps[by][hq * 64:(hq + 1) * 64, tb * P:(tb + 1) * P], lhsT=vop,
                                           rhs=ptb[:, pts[hq], col:col + P], start=(part == 0), stop=(part == 1),
                                           tile_position=(0, hq * 64))
                        for part in range(2):
                            col = (t2 * 2 + part) * P
                            ins = e.matmul(ps[bd_][hq * 64:(hq + 1) * 64, tb * P:(tb + 1) * P], lhsT=ones_bf[:, 0:64],
                                           rhs=ptb[:, pts[hq], col:col + P], start=(part == 0), stop=(part == 1),
                                           tile_position=(0, hq * 64))
                return ins
            A("pe", fn2, r=[("ptb", pts[0]), ("ptb", pts[1]), ("vcar", l), "ones"] + [("vt", b) for b in range(NBLK)],
              w=[("ps", by), ("ps", bd_)])
            if tp == 1:
                ts = nxt("tmpf", NTF)
                A("act", lambda e, cq=cq, bd_=bd_, ts=ts: e.activation(out=tmpf[:, ts, :], in_=ps[bd_][:], func=AF.Ln,
                                                                      bias=esink[:, l * 8 + cq:l * 8 + cq + 1], scale=1.0),
                  r=[("ps", bd_), "esink"], w=[("tmpf", ts)])
                A("act", lambda e, ts=ts: e.activation(out=tmpf[:, ts, :], in_=tmpf[:, ts, :], func=AF.Exp, scale=-1.0),
                  r=[("tmpf", ts)], w=[("tmpf", ts)])
                A("dve", lambda e, cq=cq, by=by, ts=ts: e.tensor_tensor(out=ya[:, cq, :], in0=ps[by][:], in1=tmpf[:, ts, :], op=ALU.mult),
                  r=[("ps", by), ("tmpf", ts)], w=[("ya", cq)])
                pump(2)

        pend = None
        for cq in range(8):
            for tp in range(2):
                pts = emit_scores(cq, tp)
                if pend is not None:
                    emit_pv(*pend)
                pend = (cq, tp, pts)
        emit_pv(*pend)
        A("dve", lambda e: e.tensor_copy(out=kcar[l][:], in_=kn[:, :, (NBLK - 1) * P:NBLK * P]),
          r=[("kn", c) for c in range(4)], w=[("kcar", l)])
        A("dve", lambda e: e.tensor_copy(out=vcar[l][:], in_=vt[:, NBLK - 1, :]), r=[("vt", NBLK - 1)], w=[("vcar", l)])

    for n in range(NT):
        for l in range(L):
            step(n, l)
    A("act", None, w=[("x", m) for m in range(KC)])
    if debug:
        for nm in dbg_list:
            A("sp", None, dma=None, r=[], w=[])
        S_.dbg_final = [("dbg_" + nm) for nm in dbg_list]

    S_.finalize()
    semkeys = set(S_.dma_val.keys()) | {e for e in S_.ENGS if S_.counts[e] > 0}
    sems = {k: es.enter_context(nc.semaphore(str(k))) for k in sorted(semkeys)}
    with nc.Block() as block:
        @block.tensor
        def _(e):
            S_.emit_engine("pe", e, sems)

        @block.scalar
        def _(e):
            S_.emit_engine("act", e, sems)

        @block.vector
        def _(e):
            S_.emit_engine("dve", e, sems)

        @block.gpsimd
        def _(e):
            S_.emit_engine("pool", e, sems)

        @block.sync
        def _(e):
            S_.emit_engine("sp", e, sems)
            for k in getattr(S_, "dbg_final", []):
                e.wait_ge(sems[k], S_.dma_val[k])
    es.close()
    nc._dbg_list = dbg_list
    return nc


def host_consts(L, ln1_g, q_norm_g, k_norm_g, sinks, conv_w, conv_b, conv_ln_g, conv_ln_b,
                sgu_ln_g, sgu_ln_b, sgu_w, sgu_b, out_norm_g, ln2_g):
    f = np.float32
    cvec = np.zeros((P, L * NCV), f)
    pidx = np.arange(P)
    for l in range(L):
        b = l * NCV
        cvec[:, b + OFF["g1"]:b + OFF["g1"] + 16] = np.asarray(ln1_g[l], f).reshape(16, P).T
        cvec[:, b + OFF["g2"]:b + OFF["g2"] + 16] = np.asarray(ln2_g[l], f).reshape(16, P).T
        cvec[:, b + OFF["go"]:b + OFF["go"] + 16] = np.asarray(out_norm_g[l], f).reshape(16, P).T
        cvec[:, b + OFF["gq"]] = np.asarray(q_norm_g[l], f)[pidx % 64]
        cvec[:, b + OFF["gk"]] = np.asarray(k_norm_g[l], f)[pidx % 64]
        sk = np.asarray(sinks[l], f)
        for c in range(8):
            cvec[:, b + OFF["sink"] + c] = sk[2 * c + pidx // 64]
        cw = np.asarray(conv_w[l], f)
        for c in range(4):
            cvec[:, b + OFF["convw"] + c * CONVK:b + OFF["convw"] + (c + 1) * CONVK] = cw[:, c * P:(c + 1) * P].T
        cvec[:, b + OFF["convb"]:b + OFF["convb"] + 4] = np.asarray(conv_b[l], f).reshape(4, P).T
        cvec[:, b + OFF["clng"]:b + OFF["clng"] + 4] = np.asarray(conv_ln_g[l], f).reshape(4, P).T
        cvec[:, b + OFF["clnb"]:b + OFF["clnb"] + 4] = np.asarray(conv_ln_b[l], f).reshape(4, P).T
    cbc = np.zeros((P, L, 2, 512), f)
    cbs = np.zeros((P, L, 4, 128), f)
    cws = np.zeros((P, L, 8, 128), f)
    for l in range(L):
        cbc[:, l, 0, :] = np.asarray(sgu_ln_g[l], f)[None, :]
        cbc[:, l, 1, :] = np.asarray(sgu_ln_b[l], f)[None, :]
        sbv = np.asarray(sgu_b[l], f)
        for c in range(4):
            cbs[:, l, c, :] = sbv[2 * c + pidx // 64, :]
        cws[:, l, :, :] = np.transpose(np.asarray(sgu_w[l], f), (2, 0, 1))
    s_ = np.arange(P)[:, None]
    q_ = np.arange(P)[None, :]
    prev = (s_ > q_).astype(f)
    cur = (q_ >= s_).astype(f)
    cmat = np.zeros((P, 1152), f)
    cmat[:, 0:512] = np.concatenate([prev, cur, prev, cur], axis=1)
    cmat[:, 512:1024] = np.concatenate([np.zeros_like(prev), cur, prev, cur], axis=1)
    perm = np.zeros((P, P), f)
    perm[(np.arange(P) + 64) % P, np.arange(P)] = 1.0
    cmat[:, 1024:1152] = perm
    return dict(cvec=cvec, cbc=cbc.reshape(P, -1), cbs=cbs.reshape(P, -1), cws=cws.reshape(P, -1), cmat=cmat)


_NC_CACHE = {}
_LAST_RES = None


def kernel(x, ln1_g, w_in, q_norm_g, k_norm_g, sinks, conv_w, conv_b, conv_ln_g, conv_ln_b,
           sgu_ln_g, sgu_ln_b, sgu_w, sgu_b, out_norm_g, w_out, ln2_g, w_up, w_down):
    x = np.asarray(x, np.float32)
    B, S, _ = x.shape
    L = int(np.asarray(w_in).shape[0])
    key = (S, L)
    if key not in _NC_CACHE:
        _NC_CACHE[key] = build_nc(S, L)
    nc = _NC_CACHE[key]
    consts = host_consts(L, ln1_g, q_norm_g, k_norm_g, sinks, conv_w, conv_b, conv_ln_g, conv_ln_b,
                         sgu_ln_g, sgu_ln_b, sgu_w, sgu_b, out_norm_g, ln2_g)
    shared = dict(w_in=np.ascontiguousarray(w_in, np.float32), w_out=np.ascontiguousarray(w_out, np.float32),
                  w_up=np.ascontiguousarray(w_up, np.float32), w_down=np.ascontiguousarray(w_down, np.float32), **consts)
    in_maps = []
    for b in range(B):
        m = dict(shared)
        m["xT"] = np.ascontiguousarray(x[b].T)
        in_maps.append(m)
    res = run_bass_kernel_spmd(nc, in_maps, core_ids=list(range(B)))
    global _LAST_RES
    _LAST_RES = res
    out = np.stack([np.ascontiguousarray(np.asarray(r["yT"]).T) for r in res.results], axis=0)
    return out.astype(np.float32)
```
